# Optimizing a Trainium2 kernel written in Bass

```python
import math
import jax
import jax.numpy as jnp
from jax import lax
import numpy as np

D_MODEL = 1024
BATCH = 8
SEQ = 4096
DEPTH = 2

GRID_W = 64
CTX_LEN = 256
EPS = 1e-6
N_MOD = 6
SSD_HEADS = 16
SSD_HEAD_DIM = 64
SSD_INNER = SSD_HEADS * SSD_HEAD_DIM
SSD_GROUPS = 2
SSD_REP = SSD_HEADS // SSD_GROUPS
SSD_STATE = 128
SSD_CONV = 5
SSD_CHUNK = 128
SSD_XBC = SSD_INNER + 2 * SSD_GROUPS * SSD_STATE
CONV_DIM = 768
CONV_WIDTH = 31
POOL_DIM = 768
POOL_WINDOWS = (2, 4, 8, 16)
POOL_GROUPS = len(POOL_WINDOWS)
POOL_GROUP_DIM = POOL_DIM // POOL_GROUPS
POOL_OUT_DIM = D_MODEL // POOL_GROUPS
N_BRANCH = 3
FFN_HIDDEN = -(-(8 * D_MODEL) // (3 * 256)) * 256
IN_WIDTHS = (SSD_INNER, SSD_XBC, 2 * SSD_HEADS, 2 * CONV_DIM, POOL_DIM, N_BRANCH * D_MODEL)
IN_DIM = sum(IN_WIDTHS)
SPLIT_POINTS = tuple(int(v) for v in np.cumsum(IN_WIDTHS)[:-1])

kernel_name = 'hybrid_ssd_conformer_pool_dit_block'


def rms_norm(x, g):
    xf = x.astype(jnp.float32)
    y = xf * lax.rsqrt(jnp.mean(xf * xf, axis=-1, keepdims=True) + EPS)
    return (y * g.astype(jnp.float32)).astype(x.dtype)


def layer_norm(x, g, b):
    xf = x.astype(jnp.float32)
    mu = jnp.mean(xf, axis=-1, keepdims=True)
    var = jnp.mean(jnp.square(xf - mu), axis=-1, keepdims=True)
    y = (xf - mu) * lax.rsqrt(var + EPS)
    return (y * g.astype(jnp.float32) + b.astype(jnp.float32)).astype(x.dtype)


def modulate(h, shift, scale):
    return h * (1 + scale) + shift


def depthwise_conv(x, w, b):
    k = w.shape[0]
    y = lax.conv_general_dilated(x, w[:, None, :].astype(x.dtype), window_strides=(1,),
                                 padding=[(k // 2, k - 1 - k // 2)],
                                 dimension_numbers=('NWC', 'WIO', 'NWC'),
                                 feature_group_count=x.shape[-1])
    return y + b


def centred_window_mean(x, window, axis):
    n = x.shape[axis]
    lo = window // 2
    hi = window - 1 - lo
    cs = jnp.cumsum(x.astype(jnp.float32), axis=axis)
    cs = jnp.concatenate([jnp.zeros_like(lax.slice_in_dim(cs, 0, 1, axis=axis)), cs], axis=axis)
    pos = np.arange(n)
    start = np.clip(pos - lo, 0, n)
    end = np.clip(pos + hi + 1, 0, n)
    total = jnp.take(cs, end, axis=axis) - jnp.take(cs, start, axis=axis)
    shape = [1] * x.ndim
    shape[axis] = n
    count = jnp.asarray((end - start).astype(np.float32).reshape(shape))
    return (total / count).astype(x.dtype)


def segsum(a):
    n = a.shape[-1]
    cs = jnp.cumsum(a, axis=-1)
    diff = cs[..., :, None] - cs[..., None, :]
    mask = np.tril(np.ones((n, n), dtype=bool))
    return jnp.where(mask, diff, -jnp.inf)


def ssd_chunked_scan(xs, da, bm, cm, init, need_y):
    nb, t = xs.shape[:2]
    nc = t // SSD_CHUNK
    xc = xs.reshape(nb, nc, SSD_CHUNK, SSD_GROUPS, SSD_REP, SSD_HEAD_DIM)
    bc = bm.reshape(nb, nc, SSD_CHUNK, SSD_GROUPS, SSD_STATE)
    cc = cm.reshape(nb, nc, SSD_CHUNK, SSD_GROUPS, SSD_STATE)
    a = da.reshape(nb, nc, SSD_CHUNK, SSD_GROUPS, SSD_REP).transpose(0, 3, 4, 1, 2)
    a_cs = jnp.cumsum(a, axis=-1)
    decay_to_end = jnp.exp(a_cs[..., -1:] - a_cs).transpose(0, 3, 4, 1, 2)
    states = jnp.einsum('bclgn,bclgrp->bcgrpn', bc, xc * decay_to_end[..., None])
    states = jnp.concatenate([init[:, None], states], axis=1)
    chunk_decay = jnp.exp(segsum(jnp.pad(a_cs[..., -1], ((0, 0), (0, 0), (0, 0), (1, 0)))))
    states = jnp.einsum('bgrzc,bcgrpn->bzgrpn', chunk_decay, states)
    prev, final = states[:, :-1], states[:, -1]
    if not need_y:
        return None, final
    decay_in = jnp.exp(segsum(a))
    cb = jnp.einsum('bclgn,bcsgn->bgcls', cc, bc)
    y_diag = jnp.einsum('bgrcls,bcsgrp->bclgrp', cb[:, :, None] * decay_in, xc)
    decay_from_start = jnp.exp(a_cs).transpose(0, 3, 4, 1, 2)
    y_off = jnp.einsum('bclgn,bcgrpn->bclgrp', cc, prev) * decay_from_start[..., None]
    y = (y_diag + y_off).reshape(nb, t, SSD_GROUPS, SSD_REP, SSD_HEAD_DIM)
    return y, final


def ssd_bidirectional(xbc, dt_raw, a_log, dt_bias, d_skip, inits, need_y):
    nb, t, _ = xbc.shape
    xbc32 = xbc.astype(jnp.float32)
    xh = xbc32[..., :SSD_INNER].reshape(nb, t, SSD_GROUPS, SSD_REP, SSD_HEAD_DIM)
    bm = xbc32[..., SSD_INNER:SSD_INNER + SSD_GROUPS * SSD_STATE].reshape(nb, t, SSD_GROUPS, SSD_STATE)
    cm = xbc32[..., SSD_INNER + SSD_GROUPS * SSD_STATE:].reshape(nb, t, SSD_GROUPS, SSD_STATE)
    dt = jax.nn.softplus(dt_raw.astype(jnp.float32).reshape(nb, t, 2, SSD_GROUPS, SSD_REP)
                         + dt_bias.astype(jnp.float32).reshape(2, SSD_GROUPS, SSD_REP))
    a = -jnp.exp(a_log.astype(jnp.float32)).reshape(2, SSD_GROUPS, SSD_REP)
    ys = []
    finals = []
    for d in range(2):
        dt_d = dt[:, :, d]
        args = (xh * dt_d[..., None], dt_d * a[d], bm, cm)
        if d == 1:
            args = tuple(jnp.flip(v, axis=1) for v in args)
        y_d, fin = ssd_chunked_scan(args[0], args[1], args[2], args[3], inits[d], need_y)
        finals.append(fin)
        if need_y:
            ys.append(y_d if d == 0 else jnp.flip(y_d, axis=1))
    if not need_y:
        return None, finals
    y = ys[0] + ys[1] + d_skip.astype(jnp.float32).reshape(SSD_GROUPS, SSD_REP, 1) * xh
    return y.reshape(nb, t, SSD_INNER).astype(xbc.dtype), finals


def pool_mixer(u, pool_w, pool_scale):
    groups = jnp.split(u, POOL_GROUPS, axis=-1)
    p = jnp.stack([centred_window_mean(g, w, 1) - g for g, w in zip(groups, POOL_WINDOWS)], axis=-2)
    y = jnp.einsum('...gc,gco->...go', p, pool_w)
    return y.reshape(y.shape[:-2] + (D_MODEL,)) * pool_scale


def token_mixer(h, rows, inits, need_out, w_in, ssd_conv_w, ssd_conv_b, ssd_a_log, ssd_dt_bias,
                ssd_d, ssd_norm_g, ssd_w_out, cv_dw_w, cv_dw_b, cv_ln_g, cv_ln_b, cv_w_out,
                pool_w, pool_scale, w_out):
    nb, t, _ = h.shape
    z, xbc, dt_raw, cv_in, pool_in, gate_logits = jnp.split(h @ w_in, SPLIT_POINTS, axis=-1)
    xbc = jax.nn.silu(depthwise_conv(xbc, ssd_conv_w, ssd_conv_b))
    y, finals = ssd_bidirectional(xbc, dt_raw, ssd_a_log, ssd_dt_bias, ssd_d, inits, need_out)
    if not need_out:
        return None, finals
    y = y * jax.nn.silu(z)
    y = rms_norm(y.reshape(nb, t, SSD_GROUPS, SSD_INNER // SSD_GROUPS),
                 ssd_norm_g.reshape(SSD_GROUPS, SSD_INNER // SSD_GROUPS)).reshape(nb, t, SSD_INNER)
    y_ssd = y @ ssd_w_out
    a, b = jnp.split(cv_in, 2, axis=-1)
    u = a * jax.nn.sigmoid(b)
    if rows is not None:
        u = u.reshape(nb * rows, GRID_W, CONV_DIM)
    u = depthwise_conv(u, cv_dw_w, cv_dw_b).reshape(nb, t, CONV_DIM)
    y_conv = jax.nn.silu(layer_norm(u, cv_ln_g, cv_ln_b)) @ cv_w_out
    if rows is not None:
        pool_in = pool_in.reshape(nb, rows, GRID_W, POOL_DIM)
    y_pool = pool_mixer(pool_in, pool_w, pool_scale).reshape(nb, t, D_MODEL)
    g = jax.nn.sigmoid(gate_logits.reshape(nb, t, N_BRANCH, D_MODEL))
    merged = g[:, :, 0] * y_ssd + g[:, :, 1] * y_conv + g[:, :, 2] * y_pool
    return merged @ w_out, finals


def swiglu(h, w_gate, w_up, w_down):
    return (jax.nn.silu(h @ w_gate) * (h @ w_up)) @ w_down


def setup_inputs(seed: int = 0) -> dict:
    key = jax.random.key(seed)
    keys = list(jax.random.split(key, 40))
    counter = [0]

    def nk():
        counter[0] += 1
        return keys[counter[0] - 1]

    def nrm(shape, scale):
        return scale * jax.random.normal(nk(), shape, jnp.float32)

    L = DEPTH
    D = D_MODEL
    dt0 = jnp.exp(jax.random.uniform(nk(), (L, 2, SSD_HEADS), jnp.float32, math.log(1e-3), math.log(1e-1)))
    return {
        'x': nrm((BATCH, SEQ, D), 1.0),
        'c': nrm((BATCH, D), 1.0),
        'ctx': nrm((BATCH, CTX_LEN, D), 1.0),
        'c_ctx': nrm((D,), 1.0),
        'w_ada': nrm((L, D, N_MOD * D), 0.5 * D ** -0.5),
        'b_ada': nrm((L, N_MOD * D), 0.02),
        'norm1_g': 1.0 + nrm((L, D), 0.02),
        'norm2_g': 1.0 + nrm((L, D), 0.02),
        'w_in': nrm((L, D, IN_DIM), D ** -0.5),
        'ssd_conv_w': nrm((L, SSD_CONV, SSD_XBC), SSD_CONV ** -0.5),
        'ssd_conv_b': nrm((L, SSD_XBC), 0.02),
        'ssd_a_log': jnp.log(jax.random.uniform(nk(), (L, 2, SSD_HEADS), jnp.float32, 1.0, 16.0)),
        'ssd_dt_bias': dt0 + jnp.log(-jnp.expm1(-dt0)),
        'ssd_d': 1.0 + nrm((L, SSD_HEADS), 0.1),
        'ssd_norm_g': 1.0 + nrm((L, SSD_INNER), 0.02),
        'ssd_w_out': nrm((L, SSD_INNER, D), SSD_INNER ** -0.5),
        'cv_dw_w': nrm((L, CONV_WIDTH, CONV_DIM), CONV_WIDTH ** -0.5),
        'cv_dw_b': nrm((L, CONV_DIM), 0.02),
        'cv_ln_g': 1.0 + nrm((L, CONV_DIM), 0.02),
        'cv_ln_b': nrm((L, CONV_DIM), 0.02),
        'cv_w_out': nrm((L, CONV_DIM, D), CONV_DIM ** -0.5),
        'pool_w': nrm((L, POOL_GROUPS, POOL_GROUP_DIM, POOL_OUT_DIM), POOL_GROUP_DIM ** -0.5),
        'pool_scale': 1.0 + nrm((L, D), 0.1),
        'w_out': nrm((L, D, D), D ** -0.5),
        'ffn_w_gate': nrm((L, D, FFN_HIDDEN), D ** -0.5),
        'ffn_w_up': nrm((L, D, FFN_HIDDEN), D ** -0.5),
        'ffn_w_down': nrm((L, FFN_HIDDEN, D), FFN_HIDDEN ** -0.5),
        'final_g': 1.0 + nrm((D,), 0.02),
    }


def reference(x, c, ctx, c_ctx, w_ada, b_ada, norm1_g, norm2_g, w_in, ssd_conv_w, ssd_conv_b,
              ssd_a_log, ssd_dt_bias, ssd_d, ssd_norm_g, ssd_w_out, cv_dw_w, cv_dw_b, cv_ln_g,
              cv_ln_b, cv_w_out, pool_w, pool_scale, w_out, ffn_w_gate, ffn_w_up, ffn_w_down,
              final_g):
    nb = x.shape[0]
    rows = x.shape[1] // GRID_W
    zero_state = jnp.zeros((nb, SSD_GROUPS, SSD_REP, SSD_HEAD_DIM, SSD_STATE), jnp.float32)
    xc = ctx
    for l in range(DEPTH):
        last = l == DEPTH - 1
        mix = dict(w_in=w_in[l], ssd_conv_w=ssd_conv_w[l], ssd_conv_b=ssd_conv_b[l],
                   ssd_a_log=ssd_a_log[l], ssd_dt_bias=ssd_dt_bias[l], ssd_d=ssd_d[l],
                   ssd_norm_g=ssd_norm_g[l], ssd_w_out=ssd_w_out[l], cv_dw_w=cv_dw_w[l],
                   cv_dw_b=cv_dw_b[l], cv_ln_g=cv_ln_g[l], cv_ln_b=cv_ln_b[l],
                   cv_w_out=cv_w_out[l], pool_w=pool_w[l], pool_scale=pool_scale[l],
                   w_out=w_out[l])
        mod_lat = jnp.split((jax.nn.silu(c) @ w_ada[l] + b_ada[l])[:, None, :], N_MOD, axis=-1)
        mod_ctx = jnp.split(jax.nn.silu(c_ctx) @ w_ada[l] + b_ada[l], N_MOD, axis=-1)
        h_ctx = modulate(rms_norm(xc, norm1_g[l]), mod_ctx[0], mod_ctx[1])
        out_ctx, ctx_states = token_mixer(h_ctx, None, (zero_state, zero_state), not last, **mix)
        h_lat = modulate(rms_norm(x, norm1_g[l]), mod_lat[0], mod_lat[1])
        out_lat, _ = token_mixer(h_lat, rows, ctx_states, True, **mix)
        x = x + mod_lat[2] * out_lat
        h_lat = modulate(rms_norm(x, norm2_g[l]), mod_lat[3], mod_lat[4])
        x = x + mod_lat[5] * swiglu(h_lat, ffn_w_gate[l], ffn_w_up[l], ffn_w_down[l])
        if not last:
            xc = xc + mod_ctx[2] * out_ctx
            h_ctx = modulate(rms_norm(xc, norm2_g[l]), mod_ctx[3], mod_ctx[4])
            xc = xc + mod_ctx[5] * swiglu(h_ctx, ffn_w_gate[l], ffn_w_up[l], ffn_w_down[l])
    return rms_norm(x, final_g)
```

```python
import contextlib
import numpy as np
import concourse.bass as bass
import concourse.mybir as mybir
from concourse.bass_utils import run_bass_kernel_spmd

F32 = mybir.dt.float32
BF16 = mybir.dt.bfloat16
AF = mybir.ActivationFunctionType
ALU = mybir.AluOpType

D = 1024
SEQ = 4096
CTXL = 256
TALL = SEQ + CTXL
NL = 2
IN_DIM = 7968
FFH = 2816
EPS = 1e-6
EPOCH = 3000
NDS = 16
NCS = 6
SAME_ENGINE_SYNC = {"act": True, "dve": True, "pool": True, "pe": False, "sp": False}

V_N1, V_N2, V_SN, V_PS, V_CB, V_CW, V_DB, V_LG, V_LB, V_DW = 0, 8, 16, 24, 32, 44, 104, 110, 116, 122
NV = 122 + 186


class Res:
    __slots__ = ("name", "writers", "readers", "base")

    def __init__(self, name=""):
        self.name = name
        self.writers = {}
        self.readers = {}
        self.base = {}


class K:
    def __init__(self, nc):
        self.nc = nc
        self.eng = dict(pe=nc.tensor, dve=nc.vector, act=nc.scalar, pool=nc.gpsimd, sp=nc.sync)
        self.sem = {}
        self.cnt = {}
        self.ep = {}
        for e in ("pe", "dve", "act", "pool"):
            self.ep[e] = 0
            self.sem[(e, 0)] = nc.alloc_semaphore("s_%s_0" % e)
            self.cnt[e] = 0
        for i in range(NDS):
            self.sem[("d", i)] = nc.alloc_semaphore("d%d" % i)
            self.cnt[("d", i)] = 0
        for i in range(NCS):
            self.sem[("c", i)] = nc.alloc_semaphore("c%d" % i)
            self.cnt[("c", i)] = 0
        self.dnext = 0
        self.cnext = 0
        self.seen = {e: {} for e in self.eng}
        self.seen_ep = {e: {} for e in self.eng}
        self.nwaits = 0
        self.ninst = 0

    def _wait(self, e, key, val):
        src, ep = key
        if src == e and not SAME_ENGINE_SYNC[e]:
            return
        if src not in ("d", "c"):
            if self.seen_ep[e].get(src, -1) > ep:
                return
        if self.seen[e].get(key, 0) >= val:
            return
        self.eng[e].wait_ge(self.sem[key], val)
        self.seen[e][key] = val
        if src not in ("d", "c") and self.seen_ep[e].get(src, -1) < ep:
            self.seen_ep[e][src] = ep
        self.nwaits += 1

    def _deps(self, e, reads, writes, appends):
        need = {}

        def add(d):
            for k_, v in d.items():
                if need.get(k_, 0) < v:
                    need[k_] = v
        for r in reads:
            add(r.writers)
        for w in writes:
            add(w.writers)
            add(w.readers)
        for a in appends:
            if a.readers:
                add(a.readers)
                add(a.writers)
            else:
                add(a.base)
        for k_ in sorted(need, key=lambda t: (str(t[0]), t[1])):
            self._wait(e, k_, need[k_])

    def _commit(self, key, val, reads, writes, appends):
        for r in reads:
            if r.readers.get(key, 0) < val:
                r.readers[key] = val
        for w in writes:
            nb = dict(w.writers)
            nb.update({k_: max(v, nb.get(k_, 0)) for k_, v in w.readers.items()})
            w.base = nb
            w.writers = {key: val}
            w.readers = {}
        for a in appends:
            if a.readers:
                nb = dict(a.writers)
                nb.update({k_: max(v, nb.get(k_, 0)) for k_, v in a.readers.items()})
                a.base = nb
                a.writers = {key: val}
                a.readers = {}
            elif a.writers.get(key, 0) < val:
                a.writers[key] = val

    def op(self, e, fn, reads=(), writes=(), appends=(), inc=True):
        self._deps(e, reads, writes, appends)
        ins = fn(self.eng[e])
        self.ninst += 1
        if inc:
            if self.cnt[e] >= EPOCH:
                self.ep[e] += 1
                self.cnt[e] = 0
                self.sem[(e, self.ep[e])] = self.nc.alloc_semaphore("s_%s_%d" % (e, self.ep[e]))
            self.cnt[e] += 1
            key = (e, self.ep[e])
            ins.then_inc(self.sem[key], 1)
            self._commit(key, self.cnt[e], reads, writes, appends)
        return ins

    def dma(self, q, out, in_, reads=(), writes=(), appends=(), cast=False, **kw):
        if cast:
            i = self.cnext
            self.cnext = (self.cnext + 1) % NCS
            key = ("c", i)
        else:
            i = self.dnext
            self.dnext = (self.dnext + 1) % NDS
            key = ("d", i)
        if self.cnt[key] > 0:
            self._wait(q, key, self.cnt[key])
        self._deps(q, reads, writes, appends)
        self.cnt[key] += 16
        ins = self.eng[q].dma_start(out=out, in_=in_, **kw)
        ins.then_inc(self.sem[key], 16)
        self.ninst += 1
        self._commit(key, self.cnt[key], reads, writes, appends)
        return ins

    def barrier(self):
        for e in self.eng:
            for src in ("pe", "dve", "act", "pool"):
                if src != e and self.cnt[src] > 0:
                    self._wait(e, (src, self.ep[src]), self.cnt[src])
            self.drain(e)

    def drain(self, e):
        for key in [("d", i) for i in range(NDS)] + [("c", i) for i in range(NCS)]:
            if self.cnt[key] > 0:
                self._wait(e, key, self.cnt[key])

    def mm(self, ps, pairs, reads, psres, extra_appends=()):
        n = len(pairs)
        self._deps("pe", reads, (), [psres] + list(extra_appends))
        for i, (l, r) in enumerate(pairs):
            last = i == n - 1
            self.op("pe", lambda e: e.matmul(ps, lhsT=l, rhs=r, start=(i == 0), stop=last),
                    reads=reads if last else (), appends=([psres] + list(extra_appends)) if last else (), inc=last)

    def act(self, out, in_, func, reads=(), writes=(), appends=(), **kw):
        return self.op("act", lambda e: e.activation(out, in_, func, **kw), reads=reads, writes=writes, appends=appends)

    def tt(self, eng, out, a, b, op, reads=(), writes=(), appends=()):
        return self.op(eng, lambda e: e.tensor_tensor(out, a, b, op), reads=reads, writes=writes, appends=appends)

    def ts(self, eng, out, a, s1, s2, op0, op1=None, reads=(), writes=(), appends=()):
        if op1 is None:
            return self.op(eng, lambda e: e.tensor_single_scalar(out, a, s1, op0), reads=reads, writes=writes, appends=appends)
        return self.op(eng, lambda e: e.tensor_scalar(out, a, s1, s2, op0, op1), reads=reads, writes=writes, appends=appends)

    def stt(self, eng, out, a, s, b, op0, op1, reads=(), writes=(), appends=()):
        return self.op(eng, lambda e: e.scalar_tensor_tensor(out, a, s, b, op0, op1), reads=reads, writes=writes, appends=appends)


class Ring:
    def __init__(self, nc, name, shape, dtype, n, psum=False):
        self.bufs = []
        for i in range(n):
            if psum:
                t = nc.alloc_psum_tensor("%s%d" % (name, i), list(shape), dtype)
            else:
                t = nc.alloc_sbuf_tensor("%s%d" % (name, i), list(shape), dtype)
            self.bufs.append((t, Res("%s%d" % (name, i))))
        self.i = 0

    def next(self):
        b = self.bufs[self.i]
        self.i = (self.i + 1) % len(self.bufs)
        return b


def bc(ap, axis, n):
    a = ap.unsqueeze(axis)
    shp = list(a.shape)
    shp[axis] = n
    return a.broadcast_to(shp)


SEGS = [dict(name="ctx", T0=0, T=CTXL, RL=CTXL, W=1, NR=CTXL, TTP=256, j=1),
        dict(name="lat", T0=CTXL, T=SEQ, RL=64, W=64, NR=64, TTP=512, j=0)]
POOLW = (2, 4, 8, 16)


def build(n_layers=NL, dbg=(), skip=()):
    nc = bass.Bass("TRN2", target_bir_lowering=False)
    k = K(nc)

    def din(name, shape, dt=F32):
        return nc.dram_tensor(name, list(shape), dt, kind="ExternalInput").ap()

    def dscr(name, shape, dt=F32):
        kind = "ExternalOutput" if name in dbg else "Internal"
        return nc.dram_tensor(name, list(shape), dt, kind=kind).ap()

    x_in = din("x", [SEQ, D])
    ctx_in = din("ctx", [CTXL, D])
    cT_in = din("cT", [128, 8, 2])
    w_ada = din("w_ada", [NL, D, 6 * D])
    b_ada = din("b_ada", [NL, 6 * D])
    b_adaT = din("b_adaT", [NL, 128, 48])
    vT_in = din("vT", [NL, 128, NV])
    a_log = din("ssd_a_log", [NL, 32])
    dt_bias = din("ssd_dt_bias", [NL, 32])
    ssd_d = din("ssd_d", [NL, 16])
    final_g = din("final_g", [D])
    w_in = din("w_in", [NL, D, IN_DIM])
    ssd_w_out = din("ssd_w_out", [NL, D, D])
    cv_w_out = din("cv_w_out", [NL, 768, D])
    pool_w = din("pool_w", [NL, 4, 192, 256])
    w_out = din("w_out", [NL, D, D])
    ffn_g = din("ffn_w_gate", [NL, D, FFH])
    ffn_u = din("ffn_w_up", [NL, D, FFH])
    ffn_d = din("ffn_w_down", [NL, FFH, D])
    out = nc.dram_tensor("out", [SEQ, D], F32, kind="ExternalOutput").ap()

    wb_in = dscr("wb_in", [NL, D, IN_DIM], BF16)
    wb_so = dscr("wb_so", [NL, D, D], BF16)
    wb_co = dscr("wb_co", [NL, 768, D], BF16)
    wb_pw = dscr("wb_pw", [NL, 768, 256], BF16)
    wb_wo = dscr("wb_wo", [NL, D, D], BF16)
    wb_fg = dscr("wb_fg", [NL, D, FFH], BF16)
    wb_fu = dscr("wb_fu", [NL, D, FFH], BF16)
    wb_fd = dscr("wb_fd", [NL, FFH, D], BF16)
    modD = dscr("modD", [NL, 2, 6 * D])
    xs = dscr("xs", [TALL, D])
    hT_d = dscr("hT", [D, TALL], BF16)
    xbcT = dscr("xbcT", [1536, TALL], BF16)
    cvuT = dscr("cvuT", [768, TALL], BF16)
    mconv = dscr("mconv", [D, TALL])
    plT = dscr("plT", [768, TALL])
    gT = dscr("gT", [3072, TALL], BF16)
    zs = dscr("zs", [TALL, D], BF16)
    dtr = dscr("dtr", [TALL, 32])
    yp = dscr("yp", [TALL, D])
    stb = dscr("stb", [TALL // 128, 128, D])
    ctd = dscr("ctd", [TALL // 128, 128, 256], BF16)
    decd = dscr("decd", [TALL // 128, 128, 32])
    ynT = dscr("ynT", [TALL // 128, 128, D], BF16)
    ppd = dscr("ppd", [768, TALL], BF16) if "ppd" in dbg else None

    NCH = TALL // 128
    R = {n: [Res("%s%d" % (n, c)) for c in range(NCH)] for n in
         "xs hT xbcT cvuT plT gT zs dtr yp stb ctd decd ynT mconv".split()}
    RWL = [{n: Res(n) for n in "wb_in wb_so wb_co wb_pw wb_wo wb_fg wb_fu wb_fd modD".split()} for _ in range(NL)]
    RW = dict(RWL[0])
    bgq = []

    def bg_step(n=1):
        for _ in range(n):
            if bgq:
                bgq.pop(0)()

    def cr(name, t0, t1):
        return [R[name][c] for c in range(t0 // 128, (t1 - 1) // 128 + 1)]

    def sb(name, shape, dt=F32):
        return nc.alloc_sbuf_tensor("sb_" + name, list(shape), dt)

    ident_f = sb("ident_f", [128, 128])
    ident_b = sb("ident_b", [128, 128], BF16)
    ones_f = sb("ones_f", [128, 128])
    m_le = sb("m_le", [128, 128])
    m_gt = sb("m_gt", [128, 128])
    m_lt = sb("m_lt", [128, 128])
    m_ge = sb("m_ge", [128, 128])
    r_const = Res("const")
    vT = sb("vT", [128, NV])
    r_vT = Res("vT")
    modT = sb("modT", [128, 48, 2])
    r_modT = Res("modT")
    abT = sb("abT", [128, 2, 4, 8])
    r_abT = Res("abT")
    fgB = sb("fgB", [128, D])
    r_fgB = Res("fgB")
    ssdc = sb("ssdc", [128, 80])
    r_ssdc = Res("ssdc")
    SS = {}
    r_Sf, r_Sb, r_Sfb, r_Sbb = Res("Sf"), Res("Sb"), Res("Sfb"), Res("Sbb")
    invc_l = sb("invc_l", [128, 4, 64])
    invc_c = sb("invc_c", [128, 4, 256])
    epsT = sb("epsT", [128, 1])

    psf = Ring(nc, "psf", [128, 512], F32, 5, psum=True)
    ps_hold = Ring(nc, "psh", [128, 512], F32, 1, psum=True)
    psb = Ring(nc, "psb", [128, 1024], BF16, 2, psum=True)

    def mk_mask(t, cmp_op, base, mult_p, step_f):
        k.op("pool", lambda e: e.memset(t[:], 1.0), writes=[r_const])
        k.op("pool", lambda e: e.affine_select(t[:], t[:], pattern=[[step_f, 128]], compare_op=cmp_op, fill=0.0,
                                               base=base, channel_multiplier=mult_p), writes=[r_const])
    mk_mask(m_le, ALU.is_ge, 0, -1, 1)
    mk_mask(m_gt, ALU.is_gt, 0, 1, -1)
    mk_mask(m_lt, ALU.is_gt, 0, -1, 1)
    mk_mask(m_ge, ALU.is_ge, 0, 1, -1)
    k.op("pool", lambda e: e.memset(ones_f[:], 1.0), writes=[r_const])
    k.op("pool", lambda e: e.memset(ident_f[:], 0.0), writes=[r_const])
    k.op("pool", lambda e: e.affine_select(ident_f[:], ident_f[:], pattern=[[-1, 128]], compare_op=ALU.not_equal,
                                           fill=1.0, base=0, channel_multiplier=1), writes=[r_const])
    k.op("dve", lambda e: e.tensor_copy(ident_b[:], ident_f[:]), reads=[r_const], writes=[r_const])
    k.op("pool", lambda e: e.memset(epsT[:], EPS), writes=[r_const])
    for (ic, n) in ((invc_l, 64), (invc_c, 256)):
        for g, w in enumerate(POOLW):
            lo = w // 2
            k.op("pool", lambda e: e.memset(ic[:, g, :], 1.0 / w), writes=[r_const])
            for r_ in list(range(0, lo)) + list(range(n - lo + 1, n)):
                cntv = min(r_ + w - 1 - lo, n - 1) - max(r_ - lo, 0) + 1
                k.op("pool", lambda e: e.memset(ic[:, g, r_:r_ + 1], 1.0 / cntv), writes=[r_const])
    k.dma("sp", fgB[:], final_g.partition_broadcast(128), writes=[r_fgB])

    k.dma("sp", xs[0:CTXL, :], ctx_in, appends=cr("xs", 0, CTXL))
    for i in range(8):
        t0 = CTXL + i * 512
        k.dma("sp", xs[t0:t0 + 512, :], x_in[i * 512:(i + 1) * 512, :], appends=cr("xs", t0, t0 + 512))

    def cast_w(dst, src, rows, name, l, bg=True):
        for r0 in range(0, rows, 256):
            r1 = min(rows, r0 + 256)
            fn = (lambda r0=r0, r1=r1: k.dma("pool", dst[l, r0:r1, :], src[l, r0:r1, :], appends=[RWL[l][name]], cast=True))
            if bg:
                bgq.append(fn)
            else:
                fn()

    def cast_layer(l):
        cast_w(wb_in, w_in, D, "wb_in", l, bg=(l > 0))
        cast_w(wb_co, cv_w_out, 768, "wb_co", l)
        cast_w(wb_so, ssd_w_out, D, "wb_so", l)
        bgq.append(lambda: k.dma("pool", wb_pw[l], pool_w[l].rearrange("g c o -> (g c) o"), appends=[RWL[l]["wb_pw"]], cast=True))
        cast_w(wb_wo, w_out, D, "wb_wo", l)
        cast_w(wb_fg, ffn_g, D, "wb_fg", l)
        cast_w(wb_fu, ffn_u, D, "wb_fu", l)
        cast_w(wb_fd, ffn_d, FFH, "wb_fd", l)

    for l in range(n_layers):
        cast_layer(l)

    class Scope:
        def __init__(self):
            self.st = contextlib.ExitStack()

        def sb(self, name, shape, dt=F32):
            return self.st.enter_context(nc.sbuf_tensor(name, list(shape), dt))

        def ring(self, name, shape, dt, n):
            rg = Ring.__new__(Ring)
            rg.bufs = [(self.sb("%s%d" % (name, i), shape, dt), Res("%s%d" % (name, i))) for i in range(n)]
            rg.i = 0
            return rg

        def close(self):
            self.st.close()

    uid = [0]

    def U(s):
        uid[0] += 1
        return "%s_%d" % (s, uid[0])

    def layer_setup(l):
        sc = Scope()
        k.dma("sp", vT[:], vT_in[l], writes=[r_vT])
        cT = sc.sb(U("cT"), [128, 8, 2])
        r_cT = Res()
        k.dma("sp", cT[:], cT_in, writes=[r_cT])
        sg = sc.sb(U("sg"), [128, 8, 2])
        r_sg = Res()
        k.act(sg[:], cT[:], AF.Silu, reads=[r_cT], writes=[r_sg])
        bT = sc.sb(U("bT"), [128, 48])
        r_bT = Res()
        k.dma("sp", bT[:], b_adaT[l], writes=[r_bT])
        brow = sc.sb(U("brow"), [2, 6 * D])
        r_brow = Res()
        k.dma("sp", brow[:], b_ada[l].partition_broadcast(2), writes=[r_brow])
        mrow = sc.sb(U("mrow"), [2, 6 * D])
        r_mrow = Res()
        war = sc.ring(U("wa"), [128, 8, 512], F32, 2)
        pT_, r_pT = ps_hold.next()
        for n in range(12):
            wa, r_wa = war.next()
            k.dma("sp", wa[:], w_ada[l, :, n * 512:(n + 1) * 512].rearrange("(kc p) n -> p kc n", p=128), writes=[r_wa])
            for j in range(4):
                cidx = n * 4 + j
                k.mm(pT_[:, cidx * 2:cidx * 2 + 2],
                     [(wa[:, kc, j * 128:(j + 1) * 128], sg[:, kc, :]) for kc in range(8)],
                     reads=[r_wa, r_sg], psres=r_pT)
            pr, r_pr = psf.next()
            k.mm(pr[0:2, :], [(sg[:, kc, :], wa[:, kc, :]) for kc in range(8)], reads=[r_wa, r_sg], psres=r_pr)
            k.tt("dve", mrow[:, n * 512:(n + 1) * 512], pr[0:2, :], brow[:, n * 512:(n + 1) * 512], ALU.add,
                 reads=[r_pr, r_brow], appends=[r_mrow])
        k.tt("dve", modT[:], pT_[:, 0:96].rearrange("p (c j) -> p c j", j=2), bc(bT[:], 2, 2), ALU.add,
             reads=[r_pT, r_bT], writes=[r_modT])
        k.dma("pool", modD[l], mrow[:], reads=[r_mrow], writes=[RW["modD"]])
        for j in range(2):
            k.stt("dve", abT[:, j, 0, :], modT[:, 8:16, j], 1.0, vT[:, V_N1:V_N1 + 8], ALU.add, ALU.mult,
                  reads=[r_modT, r_vT], appends=[r_abT])
            k.op("dve", lambda e: e.tensor_copy(abT[:, j, 1, :], modT[:, 0:8, j]), reads=[r_modT], appends=[r_abT])
            k.stt("dve", abT[:, j, 2, :], modT[:, 32:40, j], 1.0, vT[:, V_N2:V_N2 + 8], ALU.add, ALU.mult,
                  reads=[r_modT, r_vT], appends=[r_abT])
            k.op("dve", lambda e: e.tensor_copy(abT[:, j, 3, :], modT[:, 24:32, j]), reads=[r_modT], appends=[r_abT])
        tmpc = sc.sb(U("tmpc"), [128, 32])
        r_t = Res()
        k.dma("sp", tmpc[:], a_log[l].partition_broadcast(128), writes=[r_t])
        k.act(ssdc[:, 0:32], tmpc[:], AF.Exp, reads=[r_t], appends=[r_ssdc])
        k.ts("dve", ssdc[:, 0:32], ssdc[:, 0:32], -1.0, None, ALU.mult, reads=[r_ssdc], appends=[r_ssdc])
        k.dma("sp", ssdc[:, 32:64], dt_bias[l].partition_broadcast(128), appends=[r_ssdc])
        k.dma("sp", ssdc[:, 64:80], ssd_d[l].partition_broadcast(128), appends=[r_ssdc])
        return sc

    def norm_T(sc_bufs, xt_ap, r_x, j, which, hT_ap, r_hT):
        junk, r_junk, ssq, xnr = sc_bufs
        ss, r_ss = ssq.next()
        k.act(junk[:], xt_ap, AF.Square, reads=[r_x], writes=[r_junk], accum_out=ss[:, 0:1], appends=[r_ss])
        k.act(ss[:, 1:2], ss[:, 0:1], AF.Sqrt, reads=[r_ss, r_const], appends=[r_ss], scale=1.0 / D, bias=epsT[:, 0:1])
        k.op("dve", lambda e: e.reciprocal(ss[:, 2:3], ss[:, 1:2]), reads=[r_ss], appends=[r_ss])
        xn, r_xn = xnr.next()
        k.ts("dve", xn[:], xt_ap, ss[:, 2:3], None, ALU.mult, reads=[r_x, r_ss], writes=[r_xn])
        pt, r_pt = psb.next()
        k._deps("pe", [r_xn, r_const], (), [r_pt])
        for kc in range(8):
            k.op("pe", lambda e: e.transpose(pt[:, kc * 128:(kc + 1) * 128], xn[:, kc * 128:(kc + 1) * 128], ident_b[:]),
                 reads=[r_xn, r_const] if kc == 7 else (), appends=[r_pt] if kc == 7 else (), inc=(kc == 7))
        for kc in range(8):
            k.act(hT_ap(kc), pt[:, kc * 128:(kc + 1) * 128], AF.Identity, reads=[r_pt, r_abT], appends=[r_hT],
                  scale=abT[:, j, which, kc:kc + 1], bias=abT[:, j, which + 1, kc:kc + 1])

    def pass_P1(l, seg):
        sc = Scope()
        T0, T, TT, j = seg["T0"], seg["T"], seg["TTP"], seg["j"]
        NS = TT // 128
        NC1 = 2592
        w1 = sc.sb(U("w1"), [128, 8, NC1], BF16)
        r_w1 = Res()
        for kc in range(8):
            k.dma("sp", w1[:, kc, :], wb_in[l, kc * 128:(kc + 1) * 128, 0:NC1], reads=[RW["wb_in"]], appends=[r_w1])
        xtr = sc.ring(U("xt"), [128, NS, D], F32, 2)
        hTr = sc.ring(U("hTt"), [128, 8, TT], BF16, 2)
        junk = sc.sb(U("junk"), [128, D], BF16)
        nb = (junk, Res(), sc.ring(U("ss"), [128, 4], F32, 4), sc.ring(U("xn"), [128, D], BF16, 2))
        of = sc.ring(U("of"), [128, TT], BF16, 4)
        oz = sc.ring(U("oz"), [128, 512], BF16, 3)
        od = sc.ring(U("od"), [128, 32], F32, 2)
        sgm = sc.ring(U("sgm"), [128, 512], F32, 2)
        for t0 in range(T0, T0 + T, TT):
            bg_step()
            xt, r_xt = xtr.next()
            k.dma("sp", xt[:], xs[t0:t0 + TT, :].rearrange("(s p) d -> p s d", p=128), reads=cr("xs", t0, t0 + TT), writes=[r_xt])
            hT, r_hT = hTr.next()
            for s in range(NS):
                norm_T(nb, xt[:, s, :], r_xt, j, 0, lambda kc: hT[:, kc, s * 128:(s + 1) * 128], r_hT)
            for kc in range(8):
                k.dma("pool", hT_d[kc * 128:(kc + 1) * 128, t0:t0 + TT], hT[:, kc, :], reads=[r_hT], appends=cr("hT", t0, t0 + TT))
            for c in range(12):
                ps, r_ps = psf.next()
                col = 1024 + c * 128
                k.mm(ps[:, 0:TT], [(w1[:, kc, col:col + 128], hT[:, kc, :]) for kc in range(8)], reads=[r_w1, r_hT], psres=r_ps)
                o, r_o = of.next()
                k.act(o[:], ps[:, 0:TT], AF.Copy, reads=[r_ps], writes=[r_o])
                k.dma("pool", xbcT[c * 128:(c + 1) * 128, t0:t0 + TT], o[:], reads=[r_o], appends=cr("xbcT", t0, t0 + TT))
            for s in range(NS):
                ts0 = t0 + s * 128
                for half in range(2):
                    ps, r_ps = psf.next()
                    k.mm(ps[:], [(hT[:, kc, s * 128:(s + 1) * 128], w1[:, kc, half * 512:(half + 1) * 512]) for kc in range(8)],
                         reads=[r_w1, r_hT], psres=r_ps)
                    o, r_o = oz.next()
                    k.act(o[:], ps[:], AF.Silu, reads=[r_ps], writes=[r_o])
                    k.dma("pool", zs[ts0:ts0 + 128, half * 512:(half + 1) * 512], o[:], reads=[r_o], appends=cr("zs", ts0, ts0 + 128))
                ps, r_ps = psf.next()
                k.mm(ps[:, 0:32], [(hT[:, kc, s * 128:(s + 1) * 128], w1[:, kc, 2560:2592]) for kc in range(8)],
                     reads=[r_w1, r_hT], psres=r_ps)
                o, r_o = od.next()
                k.act(o[:], ps[:, 0:32], AF.Copy, reads=[r_ps], writes=[r_o])
                k.dma("pool", dtr[ts0:ts0 + 128, :], o[:], reads=[r_o], appends=cr("dtr", ts0, ts0 + 128))
        return sc

    def pass_P2(l, seg):
        sc = Scope()
        T0, T, TT = seg["T0"], seg["T"], seg["TTP"]
        C0 = 2592
        NC2 = IN_DIM - C0
        w2 = sc.sb(U("w2"), [128, 8, NC2], BF16)
        r_w2 = Res()
        for kc in range(8):
            k.dma("sp", w2[:, kc, :], wb_in[l, kc * 128:(kc + 1) * 128, C0:IN_DIM], reads=[RW["wb_in"]], appends=[r_w2])
        hTr = sc.ring(U("hTt"), [128, 8, TT], BF16, 2)
        of = sc.ring(U("of"), [128, TT], F32, 4)
        og = sc.ring(U("og"), [128, TT], BF16, 4)
        sgr = sc.ring(U("sgr"), [128, TT], F32, 2)
        for t0 in range(T0, T0 + T, TT):
            bg_step()
            hT, r_hT = hTr.next()
            k.dma("sp", hT[:], hT_d[:, t0:t0 + TT].rearrange("(kc p) t -> p kc t", p=128), reads=cr("hT", t0, t0 + TT), writes=[r_hT])
            for c in range(6):
                psb_, r_psb = psf.next()
                colb = 768 + c * 128
                k.mm(psb_[:, 0:TT], [(w2[:, kc, colb:colb + 128], hT[:, kc, :]) for kc in range(8)], reads=[r_w2, r_hT], psres=r_psb)
                sg_, r_sg_ = sgr.next()
                k.act(sg_[:], psb_[:, 0:TT], AF.Sigmoid, reads=[r_psb], writes=[r_sg_])
                psa, r_psa = psf.next()
                cola = c * 128
                k.mm(psa[:, 0:TT], [(w2[:, kc, cola:cola + 128], hT[:, kc, :]) for kc in range(8)], reads=[r_w2, r_hT], psres=r_psa)
                o, r_o = og.next()
                k.tt("dve", o[:], psa[:, 0:TT], sg_[:], ALU.mult, reads=[r_psa, r_sg_], writes=[r_o])
                k.dma("pool", cvuT[c * 128:(c + 1) * 128, t0:t0 + TT], o[:], reads=[r_o], appends=cr("cvuT", t0, t0 + TT))
            for q in range(8):
                ps, r_ps = psf.next()
                col = 1536 + q * 96
                k.mm(ps[0:96, 0:TT], [(w2[:, kc, col:col + 96], hT[:, kc, :]) for kc in range(8)], reads=[r_w2, r_hT], psres=r_ps)
                o, r_o = of.next()
                k.act(o[0:96, :], ps[0:96, 0:TT], AF.Copy, reads=[r_ps], writes=[r_o])
                k.dma("pool", plT[q * 96:(q + 1) * 96, t0:t0 + TT], o[0:96, :], reads=[r_o], appends=cr("plT", t0, t0 + TT))
            for c in range(24):
                ps, r_ps = psf.next()
                col = 2304 + c * 128
                k.mm(ps[:, 0:TT], [(w2[:, kc, col:col + 128], hT[:, kc, :]) for kc in range(8)], reads=[r_w2, r_hT], psres=r_ps)
                o, r_o = og.next()
                k.act(o[:], ps[:, 0:TT], AF.Sigmoid, reads=[r_ps], writes=[r_o])
                k.dma("pool", gT[c * 128:(c + 1) * 128, t0:t0 + TT], o[:], reads=[r_o], appends=cr("gT", t0, t0 + TT))
        return sc

    def pass_S3a(l, seg, need_y):
        sc = Scope()
        S_f, S_b, S_fb, S_bb = SS["t"]
        T0, T, TT = seg["T0"], seg["T"], 256
        nch = T // 128
        NS = TT // 128
        dtt = sc.sb(U("dtt"), [128, nch, 32])
        dta = sc.sb(U("dta"), [128, nch, 32])
        dax = sc.sb(U("dax"), [128, nch, 32])
        dal = sc.sb(U("dal"), [128, nch, 32])
        r_dt, r_da = Res(), Res()
        k.dma("sp", dtt[:], dtr[T0:T0 + T, :].rearrange("(c p) h -> p c h", p=128), reads=cr("dtr", T0, T0 + T), writes=[r_dt])
        k.tt("dve", dtt[:], dtt[:], bc(ssdc[:, 32:64], 1, nch), ALU.add, reads=[r_dt, r_ssdc], writes=[r_dt])
        r_ax = Res()
        k.act(dax[:], dtt[:], AF.Abs, reads=[r_dt], writes=[r_ax])
        k.act(dax[:], dax[:], AF.Exp, reads=[r_ax], writes=[r_ax], scale=-1.0)
        k.act(dal[:], dax[:], AF.Ln, reads=[r_ax], writes=[r_ax], bias=1.0)
        k.stt("dve", dtt[:], dtt[:], 0.0, dal[:], ALU.max, ALU.add, reads=[r_dt, r_ax], writes=[r_dt])
        k.tt("dve", dta[:], dtt[:], bc(ssdc[:, 0:32], 1, nch), ALU.mult, reads=[r_dt, r_ssdc], writes=[r_da])

        winr = sc.ring(U("win"), [128, 12, TT + 4], BF16, 2)
        dg5 = sc.sb(U("dg5"), [128, 12, 5, 128], BF16)
        r_dg5 = Res()
        for c in range(12):
            for kk in range(5):
                k.act(dg5[:, c, kk, :], ident_f[:], AF.Identity, reads=[r_const, r_vT], appends=[r_dg5],
                      scale=vT[:, V_CW + c * 5 + kk:V_CW + c * 5 + kk + 1])
        xbr = sc.ring(U("xb"), [128, 12, TT], BF16, 2)
        decr = sc.ring(U("dec"), [128, 96], F32, 2)
        wsr = sc.ring(U("ws"), [128, 64], F32, 2)
        Btr = sc.ring(U("Bt"), [128, 256], BF16, 2)
        xsr = sc.ring(U("xsm"), [128, 4, D], BF16, 2)
        cbr = sc.ring(U("cbm"), [128, 2, 2, 128], F32, 2)
        lhr = sc.ring(U("lh"), [128, 2, 16, 128], F32, 2)
        Er = sc.ring(U("E"), [128, 512], F32, 3)
        Mr = sc.ring(U("M"), [128, 2, 16, 128], BF16, 2)
        ypr = sc.ring(U("ypt"), [128, D], F32, 2)
        t1r = sc.ring(U("t1"), [128, D], F32, 2)
        str_ = sc.ring(U("st"), [128, D], F32, 2)
        edr = sc.ring(U("ed"), [128, 32], F32, 2)

        pending = [None]

        def flush():
            if pending[0] is not None:
                back(pending[0])
                pending[0] = None

        def front(t0, s, xb, r_xb):
            c_idx = (t0 + s * 128) // 128
            cl = (t0 - T0) // 128 + s
            cs = slice(s * 128, (s + 1) * 128)
            if need_y:
                lh, r_lh = lhr.next()
                k.tt("pool", lh[:, 0, :, :], bc(m_gt[:], 1, 16), bc(dta[:, cl, 0:16], 2, 128), ALU.mult, reads=[r_const, r_da], appends=[r_lh])
                k.tt("pool", lh[:, 1, :, :], bc(m_lt[:], 1, 16), bc(dta[:, cl, 16:32], 2, 128), ALU.mult, reads=[r_const, r_da], appends=[r_lh])
            pd, r_pd = psf.next()
            k._deps("pe", [r_da, r_const], (), [r_pd])
            specs = [(m_le, 0, 0), (m_gt, 0, 16), (m_lt, 16, 32), (m_ge, 16, 48)]
            for (mk, hc, oc) in specs:
                k.op("pe", lambda e: e.matmul(pd[:, oc:oc + 16], lhsT=mk[:], rhs=dta[:, cl, hc:hc + 16], start=True, stop=True), inc=False)
            k.op("pe", lambda e: e.matmul(pd[:, 64:96], lhsT=ones_f[:], rhs=dta[:, cl, :], start=True, stop=True),
                 reads=[r_da, r_const], appends=[r_pd])
            dec, r_dec = decr.next()
            k.act(dec[:], pd[:, 0:96], AF.Exp, reads=[r_pd], writes=[r_dec])
            ws, r_ws = wsr.next()
            k.op("dve", lambda e: e.tensor_copy(ws[:, 0:32], dtt[:, cl, :]), reads=[r_dt], writes=[r_ws])
            k.tt("dve", ws[:, 32:64], dtt[:, cl, :], dec[:, 16:48], ALU.mult, reads=[r_dt, r_dec], appends=[r_ws])
            px, r_px = psb.next()
            k._deps("pe", [r_xb, r_const], (), [r_px])
            for kc in range(8):
                k.op("pe", lambda e: e.transpose(px[:, kc * 128:(kc + 1) * 128], xb[:, kc, cs], ident_b[:]),
                     reads=[r_xb, r_const] if kc == 7 else (), appends=[r_px] if kc == 7 else (), inc=(kc == 7))
            pB, r_pB = psb.next()
            k._deps("pe", [r_xb, r_const], (), [r_pB])
            for g in range(2):
                k.op("pe", lambda e: e.transpose(pB[:, g * 128:(g + 1) * 128], xb[:, 8 + g, cs], ident_b[:]),
                     reads=[r_xb, r_const] if g == 1 else (), appends=[r_pB] if g == 1 else (), inc=(g == 1))
            if need_y:
                cbm, r_cbm = cbr.next()
                pc, r_pc = psf.next()
                k._deps("pe", [r_xb], (), [r_pc])
                for g in range(2):
                    k.op("pe", lambda e: e.matmul(pc[:, g * 128:(g + 1) * 128], lhsT=xb[:, 8 + g, cs], rhs=xb[:, 10 + g, cs], start=True, stop=True),
                         reads=[r_xb] if g == 1 else (), appends=[r_pc] if g == 1 else (), inc=(g == 1))
                pcv = pc[:, 0:256].rearrange("p (g t) -> p g t", g=2)
                k.tt("dve", cbm[:, 0, :, :], pcv, bc(m_le[:], 1, 2), ALU.mult, reads=[r_pc, r_const], appends=[r_cbm])
                k.tt("dve", cbm[:, 1, :, :], pcv, bc(m_ge[:], 1, 2), ALU.mult, reads=[r_pc, r_const], appends=[r_cbm])
            Bt, r_Bt = Btr.next()
            k.act(Bt[:], pB[:, 0:256], AF.Copy, reads=[r_pB], writes=[r_Bt])
            xsm, r_xs = xsr.next()
            pxv = px[:, :].rearrange("p (h d) -> p h d", d=64)
            for i in range(4):
                k.tt("dve", xsm[:, i, :].rearrange("p (h d) -> p h d", d=64), pxv, bc(ws[:, i * 16:(i + 1) * 16], 2, 64), ALU.mult,
                     reads=[r_px, r_ws], appends=[r_xs])
            t1, r_t1 = t1r.next()
            k.tt("dve", t1[:].rearrange("p (h d) -> p h d", d=64), pxv, bc(ssdc[:, 64:80], 2, 64), ALU.mult,
                 reads=[r_px, r_ssdc], writes=[r_t1])
            Mt, r_M = (None, None)
            if need_y:
                Mt, r_M = Mr.next()
                for d_ in range(2):
                    rhs_m = m_le if d_ == 0 else m_ge
                    for hb in range(4):
                        pe_, r_pe = psf.next()
                        k._deps("pe", [r_lh, r_const], (), [r_pe])
                        for hh in range(4):
                            h = hb * 4 + hh
                            k.op("pe", lambda e: e.matmul(pe_[:, hh * 128:(hh + 1) * 128], lhsT=lh[:, d_, h, :], rhs=rhs_m[:], start=True, stop=True),
                                 reads=[r_lh, r_const] if hh == 3 else (), appends=[r_pe] if hh == 3 else (), inc=(hh == 3))
                        E, r_E = Er.next()
                        k.act(E[:], pe_[:], AF.Exp, reads=[r_pe], writes=[r_E])
                        g = hb // 2
                        k.tt("dve", Mt[:, d_, hb * 4:(hb + 1) * 4, :], E[:].rearrange("p (h t) -> p h t", h=4), bc(cbm[:, d_, g, :], 1, 4), ALU.mult,
                             reads=[r_E, r_cbm], appends=[r_M])
            stt_, r_st = str_.next()
            for g in range(2):
                ps, r_ps = psf.next()
                k.mm(ps[:], [(Bt[:, g * 128:(g + 1) * 128], xsm[:, 3, g * 512:(g + 1) * 512])], reads=[r_Bt, r_xs], psres=r_ps)
                k.act(stt_[:, g * 512:(g + 1) * 512], ps[:], AF.Copy, reads=[r_ps], appends=[r_st])
            k.dma("pool", stb[c_idx], stt_[:], reads=[r_st], writes=[R["stb"][c_idx]])
            k.dma("pool", ctd[c_idx].rearrange("p (g t) -> p g t", g=2), xb[:, 10:12, cs], reads=[r_xb], writes=[R["ctd"][c_idx]])
            ed, r_ed = edr.next()
            k.op("dve", lambda e: e.tensor_copy(ed[:, 0:16], dec[:, 48:64]), reads=[r_dec], writes=[r_ed])
            k.op("dve", lambda e: e.tensor_copy(ed[:, 16:32], dec[:, 80:96]), reads=[r_dec], appends=[r_ed])
            k.dma("pool", decd[c_idx], ed[:], reads=[r_ed], writes=[R["decd"][c_idx]])
            return dict(c_idx=c_idx, cs=cs, xb=xb, r_xb=r_xb, dec=dec, r_dec=r_dec, Bt=Bt, r_Bt=r_Bt, xsm=xsm, r_xs=r_xs,
                        t1=t1, r_t1=r_t1, Mt=Mt, r_M=r_M)

        def back(c):
            c_idx, cs, xb, r_xb, dec, r_dec = c["c_idx"], c["cs"], c["xb"], c["r_xb"], c["dec"], c["r_dec"]
            Bt, r_Bt, xsm, r_xs, t1, r_t1, Mt, r_M = c["Bt"], c["r_Bt"], c["xsm"], c["r_xs"], c["t1"], c["r_t1"], c["Mt"], c["r_M"]
            if need_y:
                pyA, r_pyA = psf.next()
                pyB, r_pyB = psf.next()
                for half, (py, r_py) in enumerate(((pyA, r_pyA), (pyB, r_pyB))):
                    k._deps("pe", [r_M, r_xs], (), [r_py])
                    for hh in range(8):
                        h = half * 8 + hh
                        k.op("pe", lambda e: e.matmul(py[:, hh * 64:(hh + 1) * 64], lhsT=Mt[:, 0, h, :], rhs=xsm[:, 0, h * 64:(h + 1) * 64], start=True, stop=False), inc=False)
                        last = hh == 7
                        k.op("pe", lambda e: e.matmul(py[:, hh * 64:(hh + 1) * 64], lhsT=Mt[:, 1, h, :], rhs=xsm[:, 1, h * 64:(h + 1) * 64], start=False, stop=True),
                             reads=[r_M, r_xs] if last else (), appends=[r_py] if last else (), inc=last)
                ypt, r_ypt = ypr.next()
                for g, (py, r_py) in enumerate(((pyA, r_pyA), (pyB, r_pyB))):
                    po, r_po = psf.next()
                    k.mm(po[:], [(xb[:, 10 + g, cs], S_fb[:, g * 512:(g + 1) * 512])], reads=[r_xb, r_Sfb], psres=r_po)
                    gs = slice(g * 512, (g + 1) * 512)
                    k.tt("dve", ypt[:, gs].rearrange("p (h d) -> p h d", d=64), po[:].rearrange("p (h d) -> p h d", d=64),
                         bc(dec[:, g * 8:(g + 1) * 8], 2, 64), ALU.mult, reads=[r_po, r_dec], appends=[r_ypt])
                    k.tt("dve", ypt[:, gs], ypt[:, gs], py[:], ALU.add, reads=[r_ypt, r_py], appends=[r_ypt])
                    k.tt("dve", ypt[:, gs], ypt[:, gs], t1[:, gs], ALU.add, reads=[r_ypt, r_t1], appends=[r_ypt])
                k.dma("pool", yp[c_idx * 128:(c_idx + 1) * 128, :], ypt[:], reads=[r_ypt], writes=[R["yp"][c_idx]])
            for g in range(2):
                ps, r_ps = psf.next()
                k.mm(ps[:], [(Bt[:, g * 128:(g + 1) * 128], xsm[:, 2, g * 512:(g + 1) * 512])], reads=[r_Bt, r_xs], psres=r_ps)
                gs = slice(g * 512, (g + 1) * 512)
                k.tt("dve", S_f[:, gs].rearrange("p (h d) -> p h d", d=64), S_f[:, gs].rearrange("p (h d) -> p h d", d=64),
                     bc(dec[:, 64 + g * 8:64 + (g + 1) * 8], 2, 64), ALU.mult, reads=[r_Sf, r_dec], writes=[r_Sf])
                k.tt("dve", S_f[:, gs], S_f[:, gs], ps[:], ALU.add, reads=[r_Sf, r_ps], writes=[r_Sf])
                k.act(S_fb[:, gs], S_f[:, gs], AF.Copy, reads=[r_Sf], writes=[r_Sfb])

        for t0 in range(T0, T0 + T, TT):
            win, r_win = winr.next()
            lo = t0 - 2
            hi = t0 + TT + 2
            a0 = max(lo, T0)
            a1 = min(hi, T0 + T)
            wrote = False
            if a0 > lo:
                k.op("dve", lambda e: e.memset(win[:, :, 0:a0 - lo], 0.0), writes=[r_win])
                wrote = True
            if a1 < hi:
                k.op("dve", lambda e: e.memset(win[:, :, TT + 4 - (hi - a1):TT + 4], 0.0),
                     writes=[] if wrote else [r_win], appends=[r_win] if wrote else [])
                wrote = True
            k.dma("sp", win[:, :, a0 - lo:a1 - lo], xbcT[:, a0:a1].rearrange("(c p) t -> p c t", p=128), reads=cr("xbcT", a0, a1),
                  appends=[r_win] if wrote else [], writes=[] if wrote else [r_win])
            xb, r_xb = xbr.next()
            for c in range(12):
                ps, r_ps = psf.next()
                k.mm(ps[:, 0:TT], [(dg5[:, c, kk, :], win[:, c, kk:kk + TT]) for kk in range(5)], reads=[r_win, r_dg5], psres=r_ps)
                k.act(xb[:, c, :], ps[:, 0:TT], AF.Silu, reads=[r_ps, r_vT], appends=[r_xb], bias=vT[:, V_CB + c:V_CB + c + 1])
            for s in range(NS):
                bg_step()
                ctx_ = front(t0, s, xb, r_xb)
                flush()
                pending[0] = ctx_
        flush()
        return sc

    def pass_S3b(l, seg, need_y, with_C=False):
        sc = Scope()
        S_f, S_b, S_fb, S_bb = SS["t"]
        T0, T = seg["T0"], seg["T"]
        nch = T // 128
        cth = make_C(l, seg, sc) if with_C else iter(())
        str_ = sc.ring(U("st"), [128, D], F32, 3)
        ctr = sc.ring(U("ct"), [128, 256], BF16, 3)
        edr = sc.ring(U("ed"), [128, 32], F32, 3)
        ypr = sc.ring(U("ypt"), [128, D], F32, 3)
        zr = sc.ring(U("zt"), [128, D], BF16, 3)
        junk = sc.sb(U("junk"), [128, 512], BF16)
        r_junk = Res()
        ssr = sc.ring(U("ss"), [128, 8], F32, 3)
        ynr = sc.ring(U("yn"), [128, D], BF16, 2)
        yTr = sc.ring(U("yT"), [128, 8, 128], BF16, 2)
        pend = [None]

        def stageB(c):
            c_idx, ypt, r_ypt, ss, r_ss = c
            k.act(ss[:, 2:4], ss[:, 0:2], AF.Sqrt, reads=[r_ss, r_const], appends=[r_ss], scale=1.0 / 512, bias=epsT[:, 0:1])
            k.op("dve", lambda e: e.reciprocal(ss[:, 4:6], ss[:, 2:4]), reads=[r_ss], appends=[r_ss])
            yn, r_yn = ynr.next()
            for g in range(2):
                gs = slice(g * 512, (g + 1) * 512)
                k.ts("dve", yn[:, gs], ypt[:, gs], ss[:, 4 + g:5 + g], None, ALU.mult, reads=[r_ypt, r_ss], appends=[r_yn])
            pt, r_pt = psb.next()
            k._deps("pe", [r_yn, r_const], (), [r_pt])
            for kc in range(8):
                k.op("pe", lambda e: e.transpose(pt[:, kc * 128:(kc + 1) * 128], yn[:, kc * 128:(kc + 1) * 128], ident_b[:]),
                     reads=[r_yn, r_const] if kc == 7 else (), appends=[r_pt] if kc == 7 else (), inc=(kc == 7))
            yT, r_yT = yTr.next()
            k.tt("dve", yT[:], pt[:, :].rearrange("p (c t) -> p c t", c=8), bc(vT[:, V_SN:V_SN + 8], 2, 128), ALU.mult,
                 reads=[r_pt, r_vT], writes=[r_yT])
            k.dma("pool", ynT[c_idx], yT[:].rearrange("p c t -> p (c t)"), reads=[r_yT], writes=[R["ynT"][c_idx]])

        for cl in range(nch - 1, -1, -1):
            bg_step()
            next(cth, None)
            c_idx = T0 // 128 + cl
            stt_, r_st = str_.next()
            k.dma("sp", stt_[:], stb[c_idx], reads=[R["stb"][c_idx]], writes=[r_st])
            ed, r_ed = edr.next()
            k.dma("sp", ed[:], decd[c_idx], reads=[R["decd"][c_idx]], writes=[r_ed])
            pos = []
            if need_y:
                ct, r_ct = ctr.next()
                k.dma("sp", ct[:], ctd[c_idx], reads=[R["ctd"][c_idx]], writes=[r_ct])
                ypt, r_ypt = ypr.next()
                k.dma("sp", ypt[:], yp[c_idx * 128:(c_idx + 1) * 128, :], reads=[R["yp"][c_idx]], writes=[r_ypt])
                zt, r_zt = zr.next()
                k.dma("sp", zt[:], zs[c_idx * 128:(c_idx + 1) * 128, :], reads=[R["zs"][c_idx]], writes=[r_zt])
                for g in range(2):
                    gs = slice(g * 512, (g + 1) * 512)
                    po, r_po = psf.next()
                    k.mm(po[:], [(ct[:, g * 128:(g + 1) * 128], S_bb[:, gs])], reads=[r_ct, r_Sbb], psres=r_po)
                    pos.append((po, r_po))
            for g in range(2):
                gs = slice(g * 512, (g + 1) * 512)
                k.tt("dve", S_b[:, gs].rearrange("p (h d) -> p h d", d=64), S_b[:, gs].rearrange("p (h d) -> p h d", d=64),
                     bc(ed[:, 16 + g * 8:16 + (g + 1) * 8], 2, 64), ALU.mult, reads=[r_Sb, r_ed], writes=[r_Sb])
                k.tt("dve", S_b[:, gs], S_b[:, gs], stt_[:, gs], ALU.add, reads=[r_Sb, r_st], writes=[r_Sb])
                k.act(S_bb[:, gs], S_b[:, gs], AF.Copy, reads=[r_Sb], writes=[r_Sbb])
            if need_y:
                ss, r_ss = ssr.next()
                for g in range(2):
                    gs = slice(g * 512, (g + 1) * 512)
                    po, r_po = pos[g]
                    k.tt("dve", po[:].rearrange("p (h d) -> p h d", d=64), po[:].rearrange("p (h d) -> p h d", d=64),
                         bc(ed[:, g * 8:(g + 1) * 8], 2, 64), ALU.mult, reads=[r_ed], writes=[r_po])
                    k.tt("dve", ypt[:, gs], ypt[:, gs], po[:], ALU.add, reads=[r_po], writes=[r_ypt])
                    k.tt("dve", ypt[:, gs], ypt[:, gs], zt[:, gs], ALU.mult, reads=[r_zt], writes=[r_ypt])
                    k.act(junk[:], ypt[:, gs], AF.Square, reads=[r_ypt], writes=[r_junk], accum_out=ss[:, g:g + 1], appends=[r_ss])
                if pend[0] is not None:
                    stageB(pend[0])
                pend[0] = (c_idx, ypt, r_ypt, ss, r_ss)
        if pend[0] is not None:
            stageB(pend[0])
            pend[0] = None
        for _ in cth:
            pass
        return sc

    def make_C(l, seg, sc):
        T0, T, RL, TT = seg["T0"], seg["T"], seg["RL"], seg["TTP"]
        NRows = TT // RL
        dg = sc.sb(U("dg"), [128, 6, 31, 128], BF16)
        r_dg = Res()
        for c in range(6):
            for kk in range(31):
                k.act(dg[:, c, kk, :], ident_f[:], AF.Identity, reads=[r_const, r_vT], appends=[r_dg],
                      scale=vT[:, V_DW + c * 31 + kk:V_DW + c * 31 + kk + 1])
        cw = sc.sb(U("cw"), [128, 6, D], BF16)
        r_w = Res()
        k.dma("sp", cw[:], wb_co[l].rearrange("(kc p) n -> p kc n", p=128), reads=[RW["wb_co"]], writes=[r_w])
        cwr = sc.ring(U("cwin"), [128, 6, NRows, RL + 32], BF16, 2)
        for t_, r_ in cwr.bufs:
            k.op("pool", lambda e: e.memset(t_[:], 0.0), writes=[r_])
        acc = sc.sb(U("cacc"), [128, 6, TT], F32)
        r_acc = Res()
        sqr = sc.ring(U("sq"), [128, TT], F32, 2)
        stt_ = sc.sb(U("lnst"), [128, 4, TT], F32)
        r_ln = Res()
        sact = sc.sb(U("sact"), [128, 6, TT], BF16)
        r_sact = Res()
        gtr = sc.ring(U("gt"), [128, 8, TT], BF16, 2)
        mcr = sc.ring(U("mc"), [128, TT], F32, 3)

        def ctile(t0):
            gt, r_gt = gtr.next()
            k.dma("sp", gt[:], gT[D:2 * D, t0:t0 + TT].rearrange("(c p) t -> p c t", p=128), reads=cr("gT", t0, t0 + TT), writes=[r_gt])
            cwin, r_cw = cwr.next()
            for c in range(6):
                k.dma("sp", cwin[:, c, :, 16:16 + RL], cvuT[c * 128:(c + 1) * 128, t0:t0 + TT].rearrange("p (r w) -> p r w", w=RL),
                      reads=cr("cvuT", t0, t0 + TT), appends=[r_cw])
            for c in range(6):
                ps, r_ps = psf.next()
                k.mm(ps[:, 0:TT].rearrange("p (r w) -> p r w", w=RL),
                     [(dg[:, c, kk, :], cwin[:, c, :, kk + 1:kk + 1 + RL]) for kk in range(31)], reads=[r_cw, r_dg], psres=r_ps)
                k.act(acc[:, c, :], ps[:, 0:TT], AF.Identity, reads=[r_ps, r_vT], appends=[r_acc], bias=vT[:, V_DB + c:V_DB + c + 1])
                if c == 2:
                    yield
            p1, r_p1 = psf.next()
            k.mm(p1[:, 0:TT], [(ones_f[:], acc[:, c, :]) for c in range(6)], reads=[r_acc, r_const], psres=r_p1)
            p2, r_p2 = psf.next()
            for c in range(6):
                sq, r_sq = sqr.next()
                k.act(sq[:], acc[:, c, :], AF.Square, reads=[r_acc], writes=[r_sq])
                k.op("pe", lambda e: e.matmul(p2[:, 0:TT], lhsT=ones_f[:], rhs=sq[:], start=(c == 0), stop=(c == 5)),
                     reads=[r_sq, r_const], appends=[r_p2])
            yield
            k.act(stt_[:, 0, :], p1[:, 0:TT], AF.Copy, reads=[r_p1], writes=[r_ln], scale=1.0 / 768)
            k.tt("dve", stt_[:, 1, :], stt_[:, 0, :], stt_[:, 0, :], ALU.mult, reads=[r_ln], writes=[r_ln])
            k.stt("dve", stt_[:, 1, :], p2[:, 0:TT], 1.0 / 768, stt_[:, 1, :], ALU.mult, ALU.subtract, reads=[r_p2, r_ln], writes=[r_ln])
            k.act(stt_[:, 3, :], stt_[:, 1, :], AF.Sqrt, reads=[r_ln, r_const], writes=[r_ln], bias=epsT[:, 0:1])
            k.op("dve", lambda e: e.reciprocal(stt_[:, 2, :], stt_[:, 3, :]), reads=[r_ln], writes=[r_ln])
            for c in range(6):
                k.tt("dve", acc[:, c, :], acc[:, c, :], stt_[:, 0, :], ALU.subtract, reads=[r_ln, r_acc], writes=[r_acc])
                k.tt("dve", acc[:, c, :], acc[:, c, :], stt_[:, 2, :], ALU.mult, reads=[r_ln, r_acc], writes=[r_acc])
                k.act(sact[:, c, :], acc[:, c, :], AF.Silu, reads=[r_acc, r_vT], appends=[r_sact],
                      scale=vT[:, V_LG + c:V_LG + c + 1], bias=vT[:, V_LB + c:V_LB + c + 1])
            yield
            for jc in range(8):
                ps, r_ps = psf.next()
                k.mm(ps[:, 0:TT], [(cw[:, kc, jc * 128:(jc + 1) * 128], sact[:, kc, :]) for kc in range(6)], reads=[r_w, r_sact], psres=r_ps)
                mc, r_mc = mcr.next()
                k.tt("dve", mc[:], ps[:, 0:TT], gt[:, jc, :], ALU.mult, reads=[r_ps, r_gt], writes=[r_mc])
                k.dma("pool", mconv[jc * 128:(jc + 1) * 128, t0:t0 + TT], mc[:], reads=[r_mc], appends=cr("mconv", t0, t0 + TT))
        def allparts():
            for t0 in range(T0, T0 + T, TT):
                for _ in ctile(t0):
                    yield
                yield
        return allparts()


    def pass_M(l, seg):
        sc = Scope()
        T0, T, RL, W, NRT, j = seg["T0"], seg["T"], seg["RL"], seg["W"], seg["NR"], seg["j"]
        TT = 256
        NRows = TT // RL
        PR = TT // W
        invc = invc_l if W == 64 else invc_c
        sw = sc.sb(U("sw"), [128, 8, D], BF16)
        pw = sc.sb(U("pw"), [96, 8, 256], BF16)
        wo = sc.sb(U("wo"), [128, 8, D], BF16)
        r_w = Res()
        k.dma("sp", sw[:], wb_so[l].rearrange("(kc p) n -> p kc n", p=128), reads=[RW["wb_so"]], appends=[r_w])
        k.dma("sp", pw[:], wb_pw[l].rearrange("(q p) n -> p q n", p=96), reads=[RW["wb_pw"]], appends=[r_w])
        k.dma("sp", wo[:], wb_wo[l].rearrange("(kc p) n -> p kc n", p=128), reads=[RW["wb_wo"]], appends=[r_w])
        mcr2 = sc.ring(U("mct"), [128, 8, TT], F32, 2)
        gtr = sc.ring(U("gt"), [128, 24, TT], BF16, 2)
        pxr = sc.ring(U("px"), [96, 2, PR + 15, W], F32, 2)
        psa_ = sc.sb(U("psa"), [96, 2, PR + 15, W], F32)
        psb2 = sc.sb(U("psb2"), [96, 2, PR + 15, W], F32)
        r_pA, r_pB2 = Res(), Res()
        pp = sc.sb(U("pp"), [96, 8, TT], BF16)
        r_pp = Res()
        ynr = sc.ring(U("ynt"), [128, 2, 8, 128], BF16, 2)
        mg = sc.sb(U("mg"), [128, 8, TT], F32)
        r_mg = Res()
        tmpr = sc.ring(U("mtmp"), [128, TT], F32, 2)
        mT = sc.sb(U("mT"), [128, 8, TT], BF16)
        r_mT = Res()
        xtr = sc.ring(U("xt"), [128, 2, D], F32, 2)
        t2r = sc.ring(U("t2"), [128, 512], F32, 2)
        mB = sc.sb(U("mB"), [128, D])
        r_modB = Res()
        k.dma("sp", mB[:], modD[l, j, 2 * D:3 * D].partition_broadcast(128), reads=[RW["modD"]], writes=[r_modB])
        for t0 in range(T0, T0 + T, TT):
            gt, r_gt = gtr.next()
            k.dma("sp", gt[:], gT[:, t0:t0 + TT].rearrange("(c p) t -> p c t", p=128), reads=cr("gT", t0, t0 + TT), writes=[r_gt])
            ynt, r_ynt = ynr.next()
            k.dma("sp", ynt[:].rearrange("p n c t -> p n (c t)"), ynT[t0 // 128:t0 // 128 + 2].rearrange("n p f -> p n f"), reads=cr("ynT", t0, t0 + TT), writes=[r_ynt])
            for jc in range(8):
                ps, r_ps = psf.next()
                k.mm(ps[:, 0:TT].rearrange("p (n t) -> p n t", n=2), [(sw[:, kc, jc * 128:(jc + 1) * 128], ynt[:, :, kc, :]) for kc in range(8)], reads=[r_w, r_ynt], psres=r_ps)
                k.tt("dve", mg[:, jc, :], ps[:, 0:TT], gt[:, jc, :], ALU.mult, reads=[r_ps, r_gt], appends=[r_mg])
            mct, r_mct = mcr2.next()
            k.dma("sp", mct[:], mconv[:, t0:t0 + TT].rearrange("(c p) t -> p c t", p=128), reads=cr("mconv", t0, t0 + TT), writes=[r_mct])
            for jc in range(8):
                k.tt("dve", mg[:, jc, :], mg[:, jc, :], mct[:, jc, :], ALU.add, reads=[r_mct, r_mg], writes=[r_mg])
            prow0 = (t0 - T0) // W
            for g, w in enumerate(POOLW):
                pxb, r_px = pxr.next()
                lo_r = prow0 - 8
                hi_r = prow0 + PR + 7
                a0 = max(lo_r, 0)
                a1 = min(hi_r, NRT)
                wrote = False
                if a0 > lo_r:
                    k.op("pool", lambda e: e.memset(pxb[:, :, 0:a0 - lo_r, :], 0.0), writes=[r_px])
                    wrote = True
                if a1 < hi_r:
                    k.op("pool", lambda e: e.memset(pxb[:, :, PR + 15 - (hi_r - a1):PR + 15, :], 0.0),
                         writes=[] if wrote else [r_px], appends=[r_px] if wrote else [])
                    wrote = True
                ta, tb = T0 + a0 * W, T0 + a1 * W
                for q2 in range(2):
                    q = g * 2 + q2
                    k.dma("sp", pxb[:, q2, a0 - lo_r:a1 - lo_r, :], plT[q * 96:(q + 1) * 96, ta:tb].rearrange("p (r w) -> p r w", w=W),
                          reads=cr("plT", ta, tb), appends=[r_px] if (wrote or q2 > 0) else [], writes=[] if (wrote or q2 > 0) else [r_px])
                lo_b = 8 - w // 2
                src, r_src = pxb, r_px
                step = 1
                dsts = [(psa_, r_pA), (psb2, r_pB2)]
                di = 0
                while step < w:
                    rem = w // (2 * step)
                    hi_need = 8 + PR - w // 2 + (w - 2 * step)
                    dst, r_dst = dsts[di]
                    di ^= 1
                    k.tt("dve", dst[:, :, lo_b:hi_need, :], src[:, :, lo_b:hi_need, :], src[:, :, lo_b + step:hi_need + step, :], ALU.add,
                         reads=[r_src], writes=[r_dst])
                    src, r_src = dst, r_dst
                    step *= 2
                ic = invc[0:96, g, prow0:prow0 + PR]
                dst, r_dst = dsts[di]
                k.tt("dve", dst[:, :, 8:8 + PR, :], src[:, :, lo_b:lo_b + PR, :], bc(bc(ic, 1, 2), 3, W), ALU.mult,
                     reads=[r_src, r_const], writes=[r_dst])
                k.tt("dve", pp[:, g * 2:g * 2 + 2, :].rearrange("p q (r w) -> p q r w", w=W), dst[:, :, 8:8 + PR, :], pxb[:, :, 8:8 + PR, :], ALU.subtract,
                     reads=[r_dst, r_px], appends=[r_pp])
                if ppd is not None:
                    for q2 in range(2):
                        q = g * 2 + q2
                        k.dma("pool", ppd[q * 96:(q + 1) * 96, t0:t0 + TT], pp[:, q, :], reads=[r_pp], writes=[Res()])
                for o2 in range(2):
                    jc = g * 2 + o2
                    ps, r_ps = psf.next()
                    k.mm(ps[:, 0:TT], [(pw[:, g * 2 + q2, o2 * 128:(o2 + 1) * 128], pp[:, g * 2 + q2, :]) for q2 in range(2)],
                         reads=[r_w, r_pp], psres=r_ps)
                    tm, r_tm = tmpr.next()
                    k.stt("dve", tm[:], ps[:, 0:TT], vT[:, V_PS + jc:V_PS + jc + 1], gt[:, 16 + jc, :], ALU.mult, ALU.mult,
                          reads=[r_ps, r_gt, r_vT], writes=[r_tm])
                    k.tt("dve", mT[:, jc, :], mg[:, jc, :], tm[:], ALU.add, reads=[r_tm, r_mg], appends=[r_mT])
            xt, r_xt = xtr.next()
            k.dma("sp", xt[:], xs[t0:t0 + TT, :].rearrange("(s p) d -> p s d", p=128), reads=cr("xs", t0, t0 + TT), writes=[r_xt])
            for s in range(2):
                for half in range(2):
                    ps, r_ps = psf.next()
                    k.mm(ps[:], [(mT[:, kc, s * 128:(s + 1) * 128], wo[:, kc, half * 512:(half + 1) * 512]) for kc in range(8)],
                         reads=[r_w, r_mT], psres=r_ps)
                    t2, r_t2 = t2r.next()
                    hs = slice(half * 512, (half + 1) * 512)
                    k.tt("dve", t2[:], ps[:], mB[:, hs], ALU.mult, reads=[r_ps, r_modB], writes=[r_t2])
                    k.tt("dve", xt[:, s, hs], xt[:, s, hs], t2[:], ALU.add, reads=[r_t2], writes=[r_xt])
            k.dma("pool", xs[t0:t0 + TT, :].rearrange("(s p) d -> p s d", p=128), xt[:], reads=[r_xt], writes=cr("xs", t0, t0 + TT))
        return sc

    def pass_F(l, seg, final):
        sc = Scope()
        T0, T, j = seg["T0"], seg["T"], seg["j"]
        TT = 256
        NJ = FFH // 128
        wg = sc.sb(U("wg"), [128, 8, FFH], BF16)
        wu = sc.sb(U("wu"), [128, 8, FFH], BF16)
        wd = sc.sb(U("wd"), [128, NJ, D], BF16)
        r_w = Res()
        for kc in range(8):
            k.dma("sp", wg[:, kc, :], wb_fg[l, kc * 128:(kc + 1) * 128, :], reads=[RW["wb_fg"]], appends=[r_w])
            k.dma("sp", wu[:, kc, :], wb_fu[l, kc * 128:(kc + 1) * 128, :], reads=[RW["wb_fu"]], appends=[r_w])
        k.dma("sp", wd[:], wb_fd[l].rearrange("(kc p) n -> p kc n", p=128), reads=[RW["wb_fd"]], appends=[r_w])
        xtr = sc.ring(U("xt"), [128, 2, D], F32, 2)
        hTr = sc.ring(U("hTt"), [128, 8, TT], BF16, 2)
        junk = sc.sb(U("junk"), [128, D], BF16)
        r_junk = Res()
        nb = (junk, r_junk, sc.ring(U("ss"), [128, 4], F32, 4), sc.ring(U("xn"), [128, D], BF16, 2))
        a_t = sc.sb(U("act"), [128, NJ, TT], BF16)
        r_a = Res()
        sgr = sc.ring(U("sg"), [128, TT], F32, 2)
        t2r = sc.ring(U("t2"), [128, 512], F32, 2)
        mB = sc.sb(U("mB"), [128, D])
        r_modB = Res()
        k.dma("sp", mB[:], modD[l, j, 5 * D:6 * D].partition_broadcast(128), reads=[RW["modD"]], writes=[r_modB])
        for t0 in range(T0, T0 + T, TT):
            xt, r_xt = xtr.next()
            k.dma("sp", xt[:], xs[t0:t0 + TT, :].rearrange("(s p) d -> p s d", p=128), reads=cr("xs", t0, t0 + TT), writes=[r_xt])
            hT, r_hT = hTr.next()
            for s in range(2):
                norm_T(nb, xt[:, s, :], r_xt, j, 2, lambda kc: hT[:, kc, s * 128:(s + 1) * 128], r_hT)
            for jc in range(NJ):
                pg, r_pg = psf.next()
                k.mm(pg[:, 0:TT], [(wg[:, kc, jc * 128:(jc + 1) * 128], hT[:, kc, :]) for kc in range(8)], reads=[r_w, r_hT], psres=r_pg)
                sg_, r_sg = sgr.next()
                k.act(sg_[:], pg[:, 0:TT], AF.Silu, reads=[r_pg], writes=[r_sg])
                pu, r_pu = psf.next()
                k.mm(pu[:, 0:TT], [(wu[:, kc, jc * 128:(jc + 1) * 128], hT[:, kc, :]) for kc in range(8)], reads=[r_w, r_hT], psres=r_pu)
                k.tt("dve", a_t[:, jc, :], pu[:, 0:TT], sg_[:], ALU.mult, reads=[r_pu, r_sg], appends=[r_a])
            for s in range(2):
                for half in range(2):
                    ps, r_ps = psf.next()
                    k.mm(ps[:], [(a_t[:, jc, s * 128:(s + 1) * 128], wd[:, jc, half * 512:(half + 1) * 512]) for jc in range(NJ)],
                         reads=[r_w, r_a], psres=r_ps)
                    t2, r_t2 = t2r.next()
                    hs = slice(half * 512, (half + 1) * 512)
                    k.tt("dve", t2[:], ps[:], mB[:, hs], ALU.mult, reads=[r_ps, r_modB], writes=[r_t2])
                    k.tt("dve", xt[:, s, hs], xt[:, s, hs], t2[:], ALU.add, reads=[r_t2], writes=[r_xt])
            if not final:
                k.dma("pool", xs[t0:t0 + TT, :].rearrange("(s p) d -> p s d", p=128), xt[:], reads=[r_xt], writes=cr("xs", t0, t0 + TT))
            else:
                ssq = nb[2]
                for s in range(2):
                    ss, r_ss = ssq.next()
                    k.act(junk[:], xt[:, s, :], AF.Square, reads=[r_xt], writes=[r_junk], accum_out=ss[:, 0:1], appends=[r_ss])
                    k.act(ss[:, 1:2], ss[:, 0:1], AF.Sqrt, reads=[r_ss, r_const], appends=[r_ss], scale=1.0 / D, bias=epsT[:, 0:1])
                    k.op("dve", lambda e: e.reciprocal(ss[:, 2:3], ss[:, 1:2]), reads=[r_ss], appends=[r_ss])
                    k.stt("dve", xt[:, s, :], xt[:, s, :], ss[:, 2:3], fgB[:], ALU.mult, ALU.mult, reads=[r_ss, r_fgB], writes=[r_xt])
                o0 = t0 - T0
                k.dma("pool", out[o0:o0 + TT, :].rearrange("(s p) d -> p s d", p=128), xt[:], reads=[r_xt], writes=[Res()])
        return sc

    def run_pass(fn, *a):
        sc = fn(*a)
        k.barrier()
        sc.close()

    for l in range(n_layers):
        last = l == NL - 1
        RW.clear()
        RW.update(RWL[l])
        run_pass(layer_setup, l)
        ssc = Scope()
        S_f = ssc.sb(U("S_f"), [128, D])
        S_b = ssc.sb(U("S_b"), [128, D])
        S_fb = ssc.sb(U("S_fb"), [128, D], BF16)
        S_bb = ssc.sb(U("S_bb"), [128, D], BF16)
        SS["t"] = (S_f, S_b, S_fb, S_bb)
        k.op("dve", lambda e: e.memset(S_f[:], 0.0), writes=[r_Sf])
        k.op("dve", lambda e: e.memset(S_b[:], 0.0), writes=[r_Sb])
        k.op("dve", lambda e: e.memset(S_fb[:], 0.0), writes=[r_Sfb])
        k.op("dve", lambda e: e.memset(S_bb[:], 0.0), writes=[r_Sbb])
        cseg, lseg = SEGS
        run_pass(pass_P1, l, cseg)
        if not last:
            run_pass(pass_P2, l, cseg)
        run_pass(pass_S3a, l, cseg, not last)
        run_pass(pass_S3b, l, cseg, not last, not last)
        run_pass(pass_P1, l, lseg)
        run_pass(pass_P2, l, lseg)
        run_pass(pass_S3a, l, lseg, True)
        run_pass(pass_S3b, l, lseg, True, True)
        ssc.close()
        if not last:
            run_pass(pass_M, l, cseg)
            if "F" not in skip:
                run_pass(pass_F, l, cseg, False)
        run_pass(pass_M, l, lseg)
        if "F" not in skip:
            run_pass(pass_F, l, lseg, last)
        bg_step(len(bgq))

    for e in ("sp", "pool"):
        k.drain(e)
    return nc, k


_CACHE = {}


def _host_inputs(inputs, b):
    f = lambda a: np.ascontiguousarray(np.asarray(a, dtype=np.float32))
    c = f(inputs["c"])[b]
    cc = f(inputs["c_ctx"])
    cT = np.stack([c.reshape(8, 128).T, cc.reshape(8, 128).T], axis=-1)
    vts = []
    for l in range(NL):
        cols = []

        def fm(v):
            return f(v).reshape(-1, 128).T
        cols.append(fm(inputs["norm1_g"][l]))
        cols.append(fm(inputs["norm2_g"][l]))
        cols.append(fm(inputs["ssd_norm_g"][l]))
        cols.append(fm(inputs["pool_scale"][l]))
        cols.append(fm(inputs["ssd_conv_b"][l]))
        cw = f(inputs["ssd_conv_w"][l])
        cols.append(cw.reshape(5, 12, 128).transpose(2, 1, 0).reshape(128, 60))
        cols.append(fm(inputs["cv_dw_b"][l]))
        cols.append(fm(inputs["cv_ln_g"][l]))
        cols.append(fm(inputs["cv_ln_b"][l]))
        dw = f(inputs["cv_dw_w"][l])
        cols.append(dw.reshape(31, 6, 128).transpose(2, 1, 0).reshape(128, 186))
        vts.append(np.concatenate(cols, axis=1))
    vT = np.stack(vts, 0)
    assert vT.shape == (NL, 128, NV)
    b_adaT = f(inputs["b_ada"]).reshape(NL, 48, 128).transpose(0, 2, 1)
    m = dict(
        x=f(inputs["x"][b]), ctx=f(inputs["ctx"][b]), cT=f(cT), w_ada=f(inputs["w_ada"]), b_ada=f(inputs["b_ada"]),
        b_adaT=f(b_adaT), vT=f(vT), ssd_a_log=f(inputs["ssd_a_log"]).reshape(NL, 32),
        ssd_dt_bias=f(inputs["ssd_dt_bias"]).reshape(NL, 32), ssd_d=f(inputs["ssd_d"]), final_g=f(inputs["final_g"]),
        w_in=f(inputs["w_in"]), ssd_w_out=f(inputs["ssd_w_out"]), cv_w_out=f(inputs["cv_w_out"]), pool_w=f(inputs["pool_w"]),
        w_out=f(inputs["w_out"]), ffn_w_gate=f(inputs["ffn_w_gate"]), ffn_w_up=f(inputs["ffn_w_up"]), ffn_w_down=f(inputs["ffn_w_down"]),
    )
    return m


def kernel(**inputs):
    if "nc" not in _CACHE:
        _CACHE["nc"] = build()[0]
    nc = _CACHE["nc"]
    in_maps = [_host_inputs(inputs, b) for b in range(8)]
    res = run_bass_kernel_spmd(nc, in_maps, core_ids=list(range(8)))
    return np.stack([np.asarray(r["out"], dtype=np.float32) for r in res.results], axis=0)
```

```python
import contextlib
import numpy as np
import concourse.bass as bass
import concourse.mybir as mybir
from concourse.bass_utils import run_bass_kernel_spmd

F32 = mybir.dt.float32
BF16 = mybir.dt.bfloat16
AF = mybir.ActivationFunctionType
ALU = mybir.AluOpType

D = 1024
SEQ = 4096
CTXL = 256
TALL = SEQ + CTXL
NL = 2
IN_DIM = 7968
FFH = 2816
EPS = 1e-6
EPOCH = 3000
NDS = 16
NCS = 6
SAME_ENGINE_SYNC = {"act": True, "dve": True, "pool": True, "pe": False, "sp": False}

V_N1, V_N2, V_SN, V_PS, V_CB, V_CW, V_DB, V_LG, V_LB, V_DW = 0, 8, 16, 24, 32, 44, 104, 110, 116, 122
NV = 122 + 186


class Res:
    __slots__ = ("name", "writers", "readers", "base")

    def __init__(self, name=""):
        self.name = name
        self.writers = {}
        self.readers = {}
        self.base = {}


class K:
    def __init__(self, nc):
        self.nc = nc
        self.eng = dict(pe=nc.tensor, dve=nc.vector, act=nc.scalar, pool=nc.gpsimd, sp=nc.sync)
        self.sem = {}
        self.cnt = {}
        self.ep = {}
        for e in ("pe", "dve", "act", "pool"):
            self.ep[e] = 0
            self.sem[(e, 0)] = nc.alloc_semaphore("s_%s_0" % e)
            self.cnt[e] = 0
        for i in range(NDS):
            self.sem[("d", i)] = nc.alloc_semaphore("d%d" % i)
            self.cnt[("d", i)] = 0
        for i in range(NCS):
            self.sem[("c", i)] = nc.alloc_semaphore("c%d" % i)
            self.cnt[("c", i)] = 0
        self.dnext = 0
        self.cnext = 0
        self.seen = {e: {} for e in self.eng}
        self.seen_ep = {e: {} for e in self.eng}
        self.nwaits = 0
        self.ninst = 0

    def _wait(self, e, key, val):
        src, ep = key
        if src == e and not SAME_ENGINE_SYNC[e]:
            return
        if src not in ("d", "c"):
            if self.seen_ep[e].get(src, -1) > ep:
                return
        if self.seen[e].get(key, 0) >= val:
            return
        self.eng[e].wait_ge(self.sem[key], val)
        self.seen[e][key] = val
        if src not in ("d", "c") and self.seen_ep[e].get(src, -1) < ep:
            self.seen_ep[e][src] = ep
        self.nwaits += 1

    def _deps(self, e, reads, writes, appends):
        need = {}

        def add(d):
            for k_, v in d.items():
                if need.get(k_, 0) < v:
                    need[k_] = v
        for r in reads:
            add(r.writers)
        for w in writes:
            add(w.writers)
            add(w.readers)
        for a in appends:
            if a.readers:
                add(a.readers)
                add(a.writers)
            else:
                add(a.base)
        for k_ in sorted(need, key=lambda t: (str(t[0]), t[1])):
            self._wait(e, k_, need[k_])

    def _commit(self, key, val, reads, writes, appends):
        for r in reads:
            if r.readers.get(key, 0) < val:
                r.readers[key] = val
        for w in writes:
            nb = dict(w.writers)
            nb.update({k_: max(v, nb.get(k_, 0)) for k_, v in w.readers.items()})
            w.base = nb
            w.writers = {key: val}
            w.readers = {}
        for a in appends:
            if a.readers:
                nb = dict(a.writers)
                nb.update({k_: max(v, nb.get(k_, 0)) for k_, v in a.readers.items()})
                a.base = nb
                a.writers = {key: val}
                a.readers = {}
            elif a.writers.get(key, 0) < val:
                a.writers[key] = val

    def op(self, e, fn, reads=(), writes=(), appends=(), inc=True):
        self._deps(e, reads, writes, appends)
        ins = fn(self.eng[e])
        self.ninst += 1
        if inc:
            if self.cnt[e] >= EPOCH:
                self.ep[e] += 1
                self.cnt[e] = 0
                self.sem[(e, self.ep[e])] = self.nc.alloc_semaphore("s_%s_%d" % (e, self.ep[e]))
            self.cnt[e] += 1
            key = (e, self.ep[e])
            ins.then_inc(self.sem[key], 1)
            self._commit(key, self.cnt[e], reads, writes, appends)
        return ins

    def dma(self, q, out, in_, reads=(), writes=(), appends=(), cast=False, **kw):
        if cast:
            i = self.cnext
            self.cnext = (self.cnext + 1) % NCS
            key = ("c", i)
        else:
            i = self.dnext
            self.dnext = (self.dnext + 1) % NDS
            key = ("d", i)
        if self.cnt[key] > 0:
            self._wait(q, key, self.cnt[key])
        self._deps(q, reads, writes, appends)
        self.cnt[key] += 16
        ins = self.eng[q].dma_start(out=out, in_=in_, **kw)
        ins.then_inc(self.sem[key], 16)
        self.ninst += 1
        self._commit(key, self.cnt[key], reads, writes, appends)
        return ins

    def barrier(self):
        for e in self.eng:
            for src in ("pe", "dve", "act", "pool"):
                if src != e and self.cnt[src] > 0:
                    self._wait(e, (src, self.ep[src]), self.cnt[src])
            self.drain(e)

    def drain(self, e):
        for key in [("d", i) for i in range(NDS)] + [("c", i) for i in range(NCS)]:
            if self.cnt[key] > 0:
                self._wait(e, key, self.cnt[key])

    def mm(self, ps, pairs, reads, psres, extra_appends=()):
        n = len(pairs)
        self._deps("pe", reads, (), [psres] + list(extra_appends))
        for i, (l, r) in enumerate(pairs):
            last = i == n - 1
            self.op("pe", lambda e: e.matmul(ps, lhsT=l, rhs=r, start=(i == 0), stop=last),
                    reads=reads if last else (), appends=([psres] + list(extra_appends)) if last else (), inc=last)

    def act(self, out, in_, func, reads=(), writes=(), appends=(), **kw):
        return self.op("act", lambda e: e.activation(out, in_, func, **kw), reads=reads, writes=writes, appends=appends)

    def tt(self, eng, out, a, b, op, reads=(), writes=(), appends=()):
        return self.op(eng, lambda e: e.tensor_tensor(out, a, b, op), reads=reads, writes=writes, appends=appends)

    def ts(self, eng, out, a, s1, s2, op0, op1=None, reads=(), writes=(), appends=()):
        if op1 is None:
            return self.op(eng, lambda e: e.tensor_single_scalar(out, a, s1, op0), reads=reads, writes=writes, appends=appends)
        return self.op(eng, lambda e: e.tensor_scalar(out, a, s1, s2, op0, op1), reads=reads, writes=writes, appends=appends)

    def stt(self, eng, out, a, s, b, op0, op1, reads=(), writes=(), appends=()):
        return self.op(eng, lambda e: e.scalar_tensor_tensor(out, a, s, b, op0, op1), reads=reads, writes=writes, appends=appends)


class Ring:
    def __init__(self, nc, name, shape, dtype, n, psum=False):
        self.bufs = []
        for i in range(n):
            if psum:
                t = nc.alloc_psum_tensor("%s%d" % (name, i), list(shape), dtype)
            else:
                t = nc.alloc_sbuf_tensor("%s%d" % (name, i), list(shape), dtype)
            self.bufs.append((t, Res("%s%d" % (name, i))))
        self.i = 0

    def next(self):
        b = self.bufs[self.i]
        self.i = (self.i + 1) % len(self.bufs)
        return b


def bc(ap, axis, n):
    a = ap.unsqueeze(axis)
    shp = list(a.shape)
    shp[axis] = n
    return a.broadcast_to(shp)


SEGS = [dict(name="ctx", T0=0, T=CTXL, RL=CTXL, W=1, NR=CTXL, TTP=256, j=1),
        dict(name="lat", T0=CTXL, T=SEQ, RL=64, W=64, NR=64, TTP=512, j=0)]
POOLW = (2, 4, 8, 16)


def build(n_layers=NL, dbg=(), skip=()):
    nc = bass.Bass("TRN2", target_bir_lowering=False)
    k = K(nc)

    def din(name, shape, dt=F32):
        return nc.dram_tensor(name, list(shape), dt, kind="ExternalInput").ap()

    def dscr(name, shape, dt=F32):
        kind = "ExternalOutput" if name in dbg else "Internal"
        return nc.dram_tensor(name, list(shape), dt, kind=kind).ap()

    x_in = din("x", [SEQ, D])
    ctx_in = din("ctx", [CTXL, D])
    cT_in = din("cT", [128, 8, 2])
    w_ada = din("w_ada", [NL, D, 6 * D])
    b_ada = din("b_ada", [NL, 6 * D])
    b_adaT = din("b_adaT", [NL, 128, 48])
    vT_in = din("vT", [NL, 128, NV])
    a_log = din("ssd_a_log", [NL, 32])
    dt_bias = din("ssd_dt_bias", [NL, 32])
    ssd_d = din("ssd_d", [NL, 16])
    final_g = din("final_g", [D])
    w_in = din("w_in", [NL, D, IN_DIM])
    ssd_w_out = din("ssd_w_out", [NL, D, D])
    cv_w_out = din("cv_w_out", [NL, 768, D])
    pool_w = din("pool_w", [NL, 4, 192, 256])
    w_out = din("w_out", [NL, D, D])
    ffn_g = din("ffn_w_gate", [NL, D, FFH])
    ffn_u = din("ffn_w_up", [NL, D, FFH])
    ffn_d = din("ffn_w_down", [NL, FFH, D])
    out = nc.dram_tensor("out", [SEQ, D], F32, kind="ExternalOutput").ap()

    wb_in = dscr("wb_in", [NL, D, IN_DIM], BF16)
    wb_so = dscr("wb_so", [NL, D, D], BF16)
    wb_co = dscr("wb_co", [NL, 768, D], BF16)
    wb_pw = dscr("wb_pw", [NL, 768, 256], BF16)
    wb_wo = dscr("wb_wo", [NL, D, D], BF16)
    wb_fg = dscr("wb_fg", [NL, D, FFH], BF16)
    wb_fu = dscr("wb_fu", [NL, D, FFH], BF16)
    wb_fd = dscr("wb_fd", [NL, FFH, D], BF16)
    modD = dscr("modD", [NL, 2, 6 * D])
    xs = dscr("xs", [TALL, D])
    hT_d = dscr("hT", [D, TALL], BF16)
    xbcT = dscr("xbcT", [1536, TALL], BF16)
    cvuT = dscr("cvuT", [768, TALL], BF16)
    mconv = dscr("mconv", [D, TALL])
    plT = dscr("plT", [768, TALL])
    gT = dscr("gT", [3072, TALL], BF16)
    zs = dscr("zs", [TALL, D], BF16)
    dtr = dscr("dtr", [TALL, 32])
    yp = dscr("yp", [TALL, D])
    stb = dscr("stb", [TALL // 128, 128, D])
    ctd = dscr("ctd", [TALL // 128, 128, 256], BF16)
    decd = dscr("decd", [TALL // 128, 128, 32])
    ynT = dscr("ynT", [TALL // 128, 128, D], BF16)
    ppd = dscr("ppd", [768, TALL], BF16) if "ppd" in dbg else None

    NCH = TALL // 128
    R = {n: [Res("%s%d" % (n, c)) for c in range(NCH)] for n in
         "xs hT xbcT cvuT plT gT zs dtr yp stb ctd decd ynT mconv".split()}
    RWL = [{n: Res(n) for n in "wb_in wb_so wb_co wb_pw wb_wo wb_fg wb_fu wb_fd modD".split()} for _ in range(NL)]
    RW = dict(RWL[0])
    bgq = []

    def bg_step(n=1):
        for _ in range(n):
            if bgq:
                bgq.pop(0)()

    def cr(name, t0, t1):
        return [R[name][c] for c in range(t0 // 128, (t1 - 1) // 128 + 1)]

    def sb(name, shape, dt=F32):
        return nc.alloc_sbuf_tensor("sb_" + name, list(shape), dt)

    ident_f = sb("ident_f", [128, 128])
    ident_b = sb("ident_b", [128, 128], BF16)
    ones_f = sb("ones_f", [128, 128])
    m_le = sb("m_le", [128, 128])
    m_gt = sb("m_gt", [128, 128])
    m_lt = sb("m_lt", [128, 128])
    m_ge = sb("m_ge", [128, 128])
    r_const = Res("const")
    vT = sb("vT", [128, NV])
    r_vT = Res("vT")
    modT = sb("modT", [128, 48, 2])
    r_modT = Res("modT")
    abT = sb("abT", [128, 2, 4, 8])
    r_abT = Res("abT")
    fgB = sb("fgB", [128, D])
    r_fgB = Res("fgB")
    ssdc = sb("ssdc", [128, 80])
    r_ssdc = Res("ssdc")
    SS = {}
    r_Sf, r_Sb, r_Sfb, r_Sbb = Res("Sf"), Res("Sb"), Res("Sfb"), Res("Sbb")
    invc_l = sb("invc_l", [128, 4, 64])
    invc_c = sb("invc_c", [128, 4, 256])
    epsT = sb("epsT", [128, 1])

    psf = Ring(nc, "psf", [128, 512], F32, 5, psum=True)
    ps_hold = Ring(nc, "psh", [128, 512], F32, 1, psum=True)
    psb = Ring(nc, "psb", [128, 1024], BF16, 2, psum=True)

    def mk_mask(t, cmp_op, base, mult_p, step_f):
        k.op("pool", lambda e: e.memset(t[:], 1.0), writes=[r_const])
        k.op("pool", lambda e: e.affine_select(t[:], t[:], pattern=[[step_f, 128]], compare_op=cmp_op, fill=0.0,
                                               base=base, channel_multiplier=mult_p), writes=[r_const])
    mk_mask(m_le, ALU.is_ge, 0, -1, 1)
    mk_mask(m_gt, ALU.is_gt, 0, 1, -1)
    mk_mask(m_lt, ALU.is_gt, 0, -1, 1)
    mk_mask(m_ge, ALU.is_ge, 0, 1, -1)
    k.op("pool", lambda e: e.memset(ones_f[:], 1.0), writes=[r_const])
    k.op("pool", lambda e: e.memset(ident_f[:], 0.0), writes=[r_const])
    k.op("pool", lambda e: e.affine_select(ident_f[:], ident_f[:], pattern=[[-1, 128]], compare_op=ALU.not_equal,
                                           fill=1.0, base=0, channel_multiplier=1), writes=[r_const])
    k.op("dve", lambda e: e.tensor_copy(ident_b[:], ident_f[:]), reads=[r_const], writes=[r_const])
    k.op("pool", lambda e: e.memset(epsT[:], EPS), writes=[r_const])
    for (ic, n) in ((invc_l, 64), (invc_c, 256)):
        for g, w in enumerate(POOLW):
            lo = w // 2
            k.op("pool", lambda e: e.memset(ic[:, g, :], 1.0 / w), writes=[r_const])
            for r_ in list(range(0, lo)) + list(range(n - lo + 1, n)):
                cntv = min(r_ + w - 1 - lo, n - 1) - max(r_ - lo, 0) + 1
                k.op("pool", lambda e: e.memset(ic[:, g, r_:r_ + 1], 1.0 / cntv), writes=[r_const])
    k.dma("sp", fgB[:], final_g.partition_broadcast(128), writes=[r_fgB])

    k.dma("sp", xs[0:CTXL, :], ctx_in, appends=cr("xs", 0, CTXL))
    for i in range(8):
        t0 = CTXL + i * 512
        k.dma("sp", xs[t0:t0 + 512, :], x_in[i * 512:(i + 1) * 512, :], appends=cr("xs", t0, t0 + 512))

    def cast_w(dst, src, rows, name, l, bg=True):
        for r0 in range(0, rows, 256):
            r1 = min(rows, r0 + 256)
            fn = (lambda r0=r0, r1=r1: k.dma("pool", dst[l, r0:r1, :], src[l, r0:r1, :], appends=[RWL[l][name]], cast=True))
            if bg:
                bgq.append(fn)
            else:
                fn()

    def cast_layer(l):
        cast_w(wb_in, w_in, D, "wb_in", l, bg=(l > 0))
        cast_w(wb_co, cv_w_out, 768, "wb_co", l)
        cast_w(wb_so, ssd_w_out, D, "wb_so", l)
        bgq.append(lambda: k.dma("pool", wb_pw[l], pool_w[l].rearrange("g c o -> (g c) o"), appends=[RWL[l]["wb_pw"]], cast=True))
        cast_w(wb_wo, w_out, D, "wb_wo", l)
        cast_w(wb_fg, ffn_g, D, "wb_fg", l)
        cast_w(wb_fu, ffn_u, D, "wb_fu", l)
        cast_w(wb_fd, ffn_d, FFH, "wb_fd", l)

    for l in range(n_layers):
        cast_layer(l)

    class Scope:
        def __init__(self):
            self.st = contextlib.ExitStack()

        def sb(self, name, shape, dt=F32):
            return self.st.enter_context(nc.sbuf_tensor(name, list(shape), dt))

        def ring(self, name, shape, dt, n):
            rg = Ring.__new__(Ring)
            rg.bufs = [(self.sb("%s%d" % (name, i), shape, dt), Res("%s%d" % (name, i))) for i in range(n)]
            rg.i = 0
            return rg

        def close(self):
            self.st.close()

    uid = [0]

    def U(s):
        uid[0] += 1
        return "%s_%d" % (s, uid[0])

    def layer_setup(l):
        sc = Scope()
        k.dma("sp", vT[:], vT_in[l], writes=[r_vT])
        cT = sc.sb(U("cT"), [128, 8, 2])
        r_cT = Res()
        k.dma("sp", cT[:], cT_in, writes=[r_cT])
        sg = sc.sb(U("sg"), [128, 8, 2])
        r_sg = Res()
        k.act(sg[:], cT[:], AF.Silu, reads=[r_cT], writes=[r_sg])
        bT = sc.sb(U("bT"), [128, 48])
        r_bT = Res()
        k.dma("sp", bT[:], b_adaT[l], writes=[r_bT])
        brow = sc.sb(U("brow"), [2, 6 * D])
        r_brow = Res()
        k.dma("sp", brow[:], b_ada[l].partition_broadcast(2), writes=[r_brow])
        mrow = sc.sb(U("mrow"), [2, 6 * D])
        r_mrow = Res()
        war = sc.ring(U("wa"), [128, 8, 512], F32, 2)
        pT_, r_pT = ps_hold.next()
        for n in range(12):
            wa, r_wa = war.next()
            k.dma("sp", wa[:], w_ada[l, :, n * 512:(n + 1) * 512].rearrange("(kc p) n -> p kc n", p=128), writes=[r_wa])
            for j in range(4):
                cidx = n * 4 + j
                k.mm(pT_[:, cidx * 2:cidx * 2 + 2],
                     [(wa[:, kc, j * 128:(j + 1) * 128], sg[:, kc, :]) for kc in range(8)],
                     reads=[r_wa, r_sg], psres=r_pT)
            pr, r_pr = psf.next()
            k.mm(pr[0:2, :], [(sg[:, kc, :], wa[:, kc, :]) for kc in range(8)], reads=[r_wa, r_sg], psres=r_pr)
            k.tt("dve", mrow[:, n * 512:(n + 1) * 512], pr[0:2, :], brow[:, n * 512:(n + 1) * 512], ALU.add,
                 reads=[r_pr, r_brow], appends=[r_mrow])
        k.tt("dve", modT[:], pT_[:, 0:96].rearrange("p (c j) -> p c j", j=2), bc(bT[:], 2, 2), ALU.add,
             reads=[r_pT, r_bT], writes=[r_modT])
        k.dma("pool", modD[l], mrow[:], reads=[r_mrow], writes=[RW["modD"]])
        for j in range(2):
            k.stt("dve", abT[:, j, 0, :], modT[:, 8:16, j], 1.0, vT[:, V_N1:V_N1 + 8], ALU.add, ALU.mult,
                  reads=[r_modT, r_vT], appends=[r_abT])
            k.op("dve", lambda e: e.tensor_copy(abT[:, j, 1, :], modT[:, 0:8, j]), reads=[r_modT], appends=[r_abT])
            k.stt("dve", abT[:, j, 2, :], modT[:, 32:40, j], 1.0, vT[:, V_N2:V_N2 + 8], ALU.add, ALU.mult,
                  reads=[r_modT, r_vT], appends=[r_abT])
            k.op("dve", lambda e: e.tensor_copy(abT[:, j, 3, :], modT[:, 24:32, j]), reads=[r_modT], appends=[r_abT])
        tmpc = sc.sb(U("tmpc"), [128, 32])
        r_t = Res()
        k.dma("sp", tmpc[:], a_log[l].partition_broadcast(128), writes=[r_t])
        k.act(ssdc[:, 0:32], tmpc[:], AF.Exp, reads=[r_t], appends=[r_ssdc])
        k.ts("dve", ssdc[:, 0:32], ssdc[:, 0:32], -1.0, None, ALU.mult, reads=[r_ssdc], appends=[r_ssdc])
        k.dma("sp", ssdc[:, 32:64], dt_bias[l].partition_broadcast(128), appends=[r_ssdc])
        k.dma("sp", ssdc[:, 64:80], ssd_d[l].partition_broadcast(128), appends=[r_ssdc])
        return sc

    def norm_T(sc_bufs, xt_ap, r_x, j, which, hT_ap, r_hT):
        junk, r_junk, ssq, xnr = sc_bufs
        ss, r_ss = ssq.next()
        k.act(junk[:], xt_ap, AF.Square, reads=[r_x], writes=[r_junk], accum_out=ss[:, 0:1], appends=[r_ss])
        k.act(ss[:, 1:2], ss[:, 0:1], AF.Sqrt, reads=[r_ss, r_const], appends=[r_ss], scale=1.0 / D, bias=epsT[:, 0:1])
        k.op("dve", lambda e: e.reciprocal(ss[:, 2:3], ss[:, 1:2]), reads=[r_ss], appends=[r_ss])
        xn, r_xn = xnr.next()
        k.ts("dve", xn[:], xt_ap, ss[:, 2:3], None, ALU.mult, reads=[r_x, r_ss], writes=[r_xn])
        pt, r_pt = psb.next()
        k._deps("pe", [r_xn, r_const], (), [r_pt])
        for kc in range(8):
            k.op("pe", lambda e: e.transpose(pt[:, kc * 128:(kc + 1) * 128], xn[:, kc * 128:(kc + 1) * 128], ident_b[:]),
                 reads=[r_xn, r_const] if kc == 7 else (), appends=[r_pt] if kc == 7 else (), inc=(kc == 7))
        for kc in range(8):
            k.act(hT_ap(kc), pt[:, kc * 128:(kc + 1) * 128], AF.Identity, reads=[r_pt, r_abT], appends=[r_hT],
                  scale=abT[:, j, which, kc:kc + 1], bias=abT[:, j, which + 1, kc:kc + 1])

    def pass_P1(l, seg):
        sc = Scope()
        T0, T, TT, j = seg["T0"], seg["T"], seg["TTP"], seg["j"]
        NS = TT // 128
        NC1 = 2592
        w1 = sc.sb(U("w1"), [128, 8, NC1], BF16)
        r_w1 = Res()
        for kc in range(8):
            k.dma("sp", w1[:, kc, :], wb_in[l, kc * 128:(kc + 1) * 128, 0:NC1], reads=[RW["wb_in"]], appends=[r_w1])
        xtr = sc.ring(U("xt"), [128, NS, D], F32, 2)
        hTr = sc.ring(U("hTt"), [128, 8, TT], BF16, 2)
        junk = sc.sb(U("junk"), [128, D], BF16)
        nb = (junk, Res(), sc.ring(U("ss"), [128, 4], F32, 4), sc.ring(U("xn"), [128, D], BF16, 2))
        of = sc.ring(U("of"), [128, TT], BF16, 4)
        oz = sc.ring(U("oz"), [128, 512], BF16, 3)
        od = sc.ring(U("od"), [128, 32], F32, 2)
        sgm = sc.ring(U("sgm"), [128, 512], F32, 2)
        for t0 in range(T0, T0 + T, TT):
            bg_step()
            xt, r_xt = xtr.next()
            k.dma("sp", xt[:], xs[t0:t0 + TT, :].rearrange("(s p) d -> p s d", p=128), reads=cr("xs", t0, t0 + TT), writes=[r_xt])
            hT, r_hT = hTr.next()
            for s in range(NS):
                norm_T(nb, xt[:, s, :], r_xt, j, 0, lambda kc: hT[:, kc, s * 128:(s + 1) * 128], r_hT)
            for kc in range(8):
                k.dma("pool", hT_d[kc * 128:(kc + 1) * 128, t0:t0 + TT], hT[:, kc, :], reads=[r_hT], appends=cr("hT", t0, t0 + TT))
            for c in range(12):
                ps, r_ps = psf.next()
                col = 1024 + c * 128
                k.mm(ps[:, 0:TT], [(w1[:, kc, col:col + 128], hT[:, kc, :]) for kc in range(8)], reads=[r_w1, r_hT], psres=r_ps)
                o, r_o = of.next()
                k.act(o[:], ps[:, 0:TT], AF.Copy, reads=[r_ps], writes=[r_o])
                k.dma("pool", xbcT[c * 128:(c + 1) * 128, t0:t0 + TT], o[:], reads=[r_o], appends=cr("xbcT", t0, t0 + TT))
            for s in range(NS):
                ts0 = t0 + s * 128
                for half in range(2):
                    ps, r_ps = psf.next()
                    k.mm(ps[:], [(hT[:, kc, s * 128:(s + 1) * 128], w1[:, kc, half * 512:(half + 1) * 512]) for kc in range(8)],
                         reads=[r_w1, r_hT], psres=r_ps)
                    o, r_o = oz.next()
                    k.act(o[:], ps[:], AF.Silu, reads=[r_ps], writes=[r_o])
                    k.dma("pool", zs[ts0:ts0 + 128, half * 512:(half + 1) * 512], o[:], reads=[r_o], appends=cr("zs", ts0, ts0 + 128))
                ps, r_ps = psf.next()
                k.mm(ps[:, 0:32], [(hT[:, kc, s * 128:(s + 1) * 128], w1[:, kc, 2560:2592]) for kc in range(8)],
                     reads=[r_w1, r_hT], psres=r_ps)
                o, r_o = od.next()
                k.act(o[:], ps[:, 0:32], AF.Copy, reads=[r_ps], writes=[r_o])
                k.dma("pool", dtr[ts0:ts0 + 128, :], o[:], reads=[r_o], appends=cr("dtr", ts0, ts0 + 128))
        return sc

    def pass_P2(l, seg):
        sc = Scope()
        T0, T, TT = seg["T0"], seg["T"], seg["TTP"]
        C0 = 2592
        NC2 = IN_DIM - C0
        w2 = sc.sb(U("w2"), [128, 8, NC2], BF16)
        r_w2 = Res()
        for kc in range(8):
            k.dma("sp", w2[:, kc, :], wb_in[l, kc * 128:(kc + 1) * 128, C0:IN_DIM], reads=[RW["wb_in"]], appends=[r_w2])
        hTr = sc.ring(U("hTt"), [128, 8, TT], BF16, 2)
        of = sc.ring(U("of"), [128, TT], F32, 4)
        og = sc.ring(U("og"), [128, TT], BF16, 4)
        sgr = sc.ring(U("sgr"), [128, TT], F32, 2)
        for t0 in range(T0, T0 + T, TT):
            bg_step()
            hT, r_hT = hTr.next()
            k.dma("sp", hT[:], hT_d[:, t0:t0 + TT].rearrange("(kc p) t -> p kc t", p=128), reads=cr("hT", t0, t0 + TT), writes=[r_hT])
            for c in range(6):
                psb_, r_psb = psf.next()
                colb = 768 + c * 128
                k.mm(psb_[:, 0:TT], [(w2[:, kc, colb:colb + 128], hT[:, kc, :]) for kc in range(8)], reads=[r_w2, r_hT], psres=r_psb)
                sg_, r_sg_ = sgr.next()
                k.act(sg_[:], psb_[:, 0:TT], AF.Sigmoid, reads=[r_psb], writes=[r_sg_])
                psa, r_psa = psf.next()
                cola = c * 128
                k.mm(psa[:, 0:TT], [(w2[:, kc, cola:cola + 128], hT[:, kc, :]) for kc in range(8)], reads=[r_w2, r_hT], psres=r_psa)
                o, r_o = og.next()
                k.tt("dve", o[:], psa[:, 0:TT], sg_[:], ALU.mult, reads=[r_psa, r_sg_], writes=[r_o])
                k.dma("pool", cvuT[c * 128:(c + 1) * 128, t0:t0 + TT], o[:], reads=[r_o], appends=cr("cvuT", t0, t0 + TT))
            for q in range(8):
                ps, r_ps = psf.next()
                col = 1536 + q * 96
                k.mm(ps[0:96, 0:TT], [(w2[:, kc, col:col + 96], hT[:, kc, :]) for kc in range(8)], reads=[r_w2, r_hT], psres=r_ps)
                o, r_o = of.next()
                k.act(o[0:96, :], ps[0:96, 0:TT], AF.Copy, reads=[r_ps], writes=[r_o])
                k.dma("pool", plT[q * 96:(q + 1) * 96, t0:t0 + TT], o[0:96, :], reads=[r_o], appends=cr("plT", t0, t0 + TT))
            for c in range(24):
                ps, r_ps = psf.next()
                col = 2304 + c * 128
                k.mm(ps[:, 0:TT], [(w2[:, kc, col:col + 128], hT[:, kc, :]) for kc in range(8)], reads=[r_w2, r_hT], psres=r_ps)
                o, r_o = og.next()
                k.act(o[:], ps[:, 0:TT], AF.Sigmoid, reads=[r_ps], writes=[r_o])
                k.dma("pool", gT[c * 128:(c + 1) * 128, t0:t0 + TT], o[:], reads=[r_o], appends=cr("gT", t0, t0 + TT))
        return sc

    def pass_S3a(l, seg, need_y):
        sc = Scope()
        S_f, S_b, S_fb, S_bb = SS["t"]
        T0, T, TT = seg["T0"], seg["T"], 256
        nch = T // 128
        NS = TT // 128
        dtt = sc.sb(U("dtt"), [128, nch, 32])
        dta = sc.sb(U("dta"), [128, nch, 32])
        dax = sc.sb(U("dax"), [128, nch, 32])
        dal = sc.sb(U("dal"), [128, nch, 32])
        r_dt, r_da = Res(), Res()
        k.dma("sp", dtt[:], dtr[T0:T0 + T, :].rearrange("(c p) h -> p c h", p=128), reads=cr("dtr", T0, T0 + T), writes=[r_dt])
        k.tt("dve", dtt[:], dtt[:], bc(ssdc[:, 32:64], 1, nch), ALU.add, reads=[r_dt, r_ssdc], writes=[r_dt])
        r_ax = Res()
        k.act(dax[:], dtt[:], AF.Abs, reads=[r_dt], writes=[r_ax])
        k.act(dax[:], dax[:], AF.Exp, reads=[r_ax], writes=[r_ax], scale=-1.0)
        k.act(dal[:], dax[:], AF.Ln, reads=[r_ax], writes=[r_ax], bias=1.0)
        k.stt("dve", dtt[:], dtt[:], 0.0, dal[:], ALU.max, ALU.add, reads=[r_dt, r_ax], writes=[r_dt])
        k.tt("dve", dta[:], dtt[:], bc(ssdc[:, 0:32], 1, nch), ALU.mult, reads=[r_dt, r_ssdc], writes=[r_da])

        winr = sc.ring(U("win"), [128, 12, TT + 4], BF16, 2)
        dg5 = sc.sb(U("dg5"), [128, 12, 5, 128], BF16)
        r_dg5 = Res()
        for c in range(12):
            for kk in range(5):
                k.act(dg5[:, c, kk, :], ident_f[:], AF.Identity, reads=[r_const, r_vT], appends=[r_dg5],
                      scale=vT[:, V_CW + c * 5 + kk:V_CW + c * 5 + kk + 1])
        xbr = sc.ring(U("xb"), [128, 12, TT], BF16, 2)
        decr = sc.ring(U("dec"), [128, 96], F32, 2)
        wsr = sc.ring(U("ws"), [128, 64], F32, 2)
        Btr = sc.ring(U("Bt"), [128, 256], BF16, 2)
        xsr = sc.ring(U("xsm"), [128, 4, D], BF16, 2)
        cbr = sc.ring(U("cbm"), [128, 2, 2, 128], F32, 2)
        lhr = sc.ring(U("lh"), [128, 2, 16, 128], F32, 2)
        Er = sc.ring(U("E"), [128, 512], F32, 3)
        Mr = sc.ring(U("M"), [128, 2, 16, 128], BF16, 2)
        ypr = sc.ring(U("ypt"), [128, D], F32, 2)
        t1r = sc.ring(U("t1"), [128, D], F32, 2)
        str_ = sc.ring(U("st"), [128, D], F32, 2)
        edr = sc.ring(U("ed"), [128, 32], F32, 2)

        pending = [None]

        def flush():
            if pending[0] is not None:
                back(pending[0])
                pending[0] = None

        def front(t0, s, xb, r_xb):
            c_idx = (t0 + s * 128) // 128
            cl = (t0 - T0) // 128 + s
            cs = slice(s * 128, (s + 1) * 128)
            if need_y:
                lh, r_lh = lhr.next()
                k.tt("pool", lh[:, 0, :, :], bc(m_gt[:], 1, 16), bc(dta[:, cl, 0:16], 2, 128), ALU.mult, reads=[r_const, r_da], appends=[r_lh])
                k.tt("pool", lh[:, 1, :, :], bc(m_lt[:], 1, 16), bc(dta[:, cl, 16:32], 2, 128), ALU.mult, reads=[r_const, r_da], appends=[r_lh])
            pd, r_pd = psf.next()
            k._deps("pe", [r_da, r_const], (), [r_pd])
            specs = [(m_le, 0, 0), (m_gt, 0, 16), (m_lt, 16, 32), (m_ge, 16, 48)]
            for (mk, hc, oc) in specs:
                k.op("pe", lambda e: e.matmul(pd[:, oc:oc + 16], lhsT=mk[:], rhs=dta[:, cl, hc:hc + 16], start=True, stop=True), inc=False)
            k.op("pe", lambda e: e.matmul(pd[:, 64:96], lhsT=ones_f[:], rhs=dta[:, cl, :], start=True, stop=True),
                 reads=[r_da, r_const], appends=[r_pd])
            dec, r_dec = decr.next()
            k.act(dec[:], pd[:, 0:96], AF.Exp, reads=[r_pd], writes=[r_dec])
            ws, r_ws = wsr.next()
            k.op("dve", lambda e: e.tensor_copy(ws[:, 0:32], dtt[:, cl, :]), reads=[r_dt], writes=[r_ws])
            k.tt("dve", ws[:, 32:64], dtt[:, cl, :], dec[:, 16:48], ALU.mult, reads=[r_dt, r_dec], appends=[r_ws])
            px, r_px = psb.next()
            k._deps("pe", [r_xb, r_const], (), [r_px])
            for kc in range(8):
                k.op("pe", lambda e: e.transpose(px[:, kc * 128:(kc + 1) * 128], xb[:, kc, cs], ident_b[:]),
                     reads=[r_xb, r_const] if kc == 7 else (), appends=[r_px] if kc == 7 else (), inc=(kc == 7))
            pB, r_pB = psb.next()
            k._deps("pe", [r_xb, r_const], (), [r_pB])
            for g in range(2):
                k.op("pe", lambda e: e.transpose(pB[:, g * 128:(g + 1) * 128], xb[:, 8 + g, cs], ident_b[:]),
                     reads=[r_xb, r_const] if g == 1 else (), appends=[r_pB] if g == 1 else (), inc=(g == 1))
            if need_y:
                cbm, r_cbm = cbr.next()
                pc, r_pc = psf.next()
                k._deps("pe", [r_xb], (), [r_pc])
                for g in range(2):
                    k.op("pe", lambda e: e.matmul(pc[:, g * 128:(g + 1) * 128], lhsT=xb[:, 8 + g, cs], rhs=xb[:, 10 + g, cs], start=True, stop=True),
                         reads=[r_xb] if g == 1 else (), appends=[r_pc] if g == 1 else (), inc=(g == 1))
                pcv = pc[:, 0:256].rearrange("p (g t) -> p g t", g=2)
                k.tt("dve", cbm[:, 0, :, :], pcv, bc(m_le[:], 1, 2), ALU.mult, reads=[r_pc, r_const], appends=[r_cbm])
                k.tt("dve", cbm[:, 1, :, :], pcv, bc(m_ge[:], 1, 2), ALU.mult, reads=[r_pc, r_const], appends=[r_cbm])
            Bt, r_Bt = Btr.next()
            k.act(Bt[:], pB[:, 0:256], AF.Copy, reads=[r_pB], writes=[r_Bt])
            xsm, r_xs = xsr.next()
            pxv = px[:, :].rearrange("p (h d) -> p h d", d=64)
            for i in range(4):
                k.tt("dve", xsm[:, i, :].rearrange("p (h d) -> p h d", d=64), pxv, bc(ws[:, i * 16:(i + 1) * 16], 2, 64), ALU.mult,
                     reads=[r_px, r_ws], appends=[r_xs])
            t1, r_t1 = t1r.next()
            k.tt("dve", t1[:].rearrange("p (h d) -> p h d", d=64), pxv, bc(ssdc[:, 64:80], 2, 64), ALU.mult,
                 reads=[r_px, r_ssdc], writes=[r_t1])
            Mt, r_M = (None, None)
            if need_y:
                Mt, r_M = Mr.next()
                for d_ in range(2):
                    rhs_m = m_le if d_ == 0 else m_ge
                    for hb in range(4):
                        pe_, r_pe = psf.next()
                        k._deps("pe", [r_lh, r_const], (), [r_pe])
                        for hh in range(4):
                            h = hb * 4 + hh
                            k.op("pe", lambda e: e.matmul(pe_[:, hh * 128:(hh + 1) * 128], lhsT=lh[:, d_, h, :], rhs=rhs_m[:], start=True, stop=True),
                                 reads=[r_lh, r_const] if hh == 3 else (), appends=[r_pe] if hh == 3 else (), inc=(hh == 3))
                        E, r_E = Er.next()
                        k.act(E[:], pe_[:], AF.Exp, reads=[r_pe], writes=[r_E])
                        g = hb // 2
                        k.tt("dve", Mt[:, d_, hb * 4:(hb + 1) * 4, :], E[:].rearrange("p (h t) -> p h t", h=4), bc(cbm[:, d_, g, :], 1, 4), ALU.mult,
                             reads=[r_E, r_cbm], appends=[r_M])
            stt_, r_st = str_.next()
            for g in range(2):
                ps, r_ps = psf.next()
                k.mm(ps[:], [(Bt[:, g * 128:(g + 1) * 128], xsm[:, 3, g * 512:(g + 1) * 512])], reads=[r_Bt, r_xs], psres=r_ps)
                k.act(stt_[:, g * 512:(g + 1) * 512], ps[:], AF.Copy, reads=[r_ps], appends=[r_st])
            k.dma("pool", stb[c_idx], stt_[:], reads=[r_st], writes=[R["stb"][c_idx]])
            k.dma("pool", ctd[c_idx].rearrange("p (g t) -> p g t", g=2), xb[:, 10:12, cs], reads=[r_xb], writes=[R["ctd"][c_idx]])
            ed, r_ed = edr.next()
            k.op("dve", lambda e: e.tensor_copy(ed[:, 0:16], dec[:, 48:64]), reads=[r_dec], writes=[r_ed])
            k.op("dve", lambda e: e.tensor_copy(ed[:, 16:32], dec[:, 80:96]), reads=[r_dec], appends=[r_ed])
            k.dma("pool", decd[c_idx], ed[:], reads=[r_ed], writes=[R["decd"][c_idx]])
            return dict(c_idx=c_idx, cs=cs, xb=xb, r_xb=r_xb, dec=dec, r_dec=r_dec, Bt=Bt, r_Bt=r_Bt, xsm=xsm, r_xs=r_xs,
                        t1=t1, r_t1=r_t1, Mt=Mt, r_M=r_M)

        def back(c):
            c_idx, cs, xb, r_xb, dec, r_dec = c["c_idx"], c["cs"], c["xb"], c["r_xb"], c["dec"], c["r_dec"]
            Bt, r_Bt, xsm, r_xs, t1, r_t1, Mt, r_M = c["Bt"], c["r_Bt"], c["xsm"], c["r_xs"], c["t1"], c["r_t1"], c["Mt"], c["r_M"]
            if need_y:
                pyA, r_pyA = psf.next()
                pyB, r_pyB = psf.next()
                for half, (py, r_py) in enumerate(((pyA, r_pyA), (pyB, r_pyB))):
                    k._deps("pe", [r_M, r_xs], (), [r_py])
                    for hh in range(8):
                        h = half * 8 + hh
                        k.op("pe", lambda e: e.matmul(py[:, hh * 64:(hh + 1) * 64], lhsT=Mt[:, 0, h, :], rhs=xsm[:, 0, h * 64:(h + 1) * 64], start=True, stop=False), inc=False)
                        last = hh == 7
                        k.op("pe", lambda e: e.matmul(py[:, hh * 64:(hh + 1) * 64], lhsT=Mt[:, 1, h, :], rhs=xsm[:, 1, h * 64:(h + 1) * 64], start=False, stop=True),
                             reads=[r_M, r_xs] if last else (), appends=[r_py] if last else (), inc=last)
                ypt, r_ypt = ypr.next()
                for g, (py, r_py) in enumerate(((pyA, r_pyA), (pyB, r_pyB))):
                    po, r_po = psf.next()
                    k.mm(po[:], [(xb[:, 10 + g, cs], S_fb[:, g * 512:(g + 1) * 512])], reads=[r_xb, r_Sfb], psres=r_po)
                    gs = slice(g * 512, (g + 1) * 512)
                    k.tt("dve", ypt[:, gs].rearrange("p (h d) -> p h d", d=64), po[:].rearrange("p (h d) -> p h d", d=64),
                         bc(dec[:, g * 8:(g + 1) * 8], 2, 64), ALU.mult, reads=[r_po, r_dec], appends=[r_ypt])
                    k.tt("dve", ypt[:, gs], ypt[:, gs], py[:], ALU.add, reads=[r_ypt, r_py], appends=[r_ypt])
                    k.tt("dve", ypt[:, gs], ypt[:, gs], t1[:, gs], ALU.add, reads=[r_ypt, r_t1], appends=[r_ypt])
                k.dma("pool", yp[c_idx * 128:(c_idx + 1) * 128, :], ypt[:], reads=[r_ypt], writes=[R["yp"][c_idx]])
            for g in range(2):
                ps, r_ps = psf.next()
                k.mm(ps[:], [(Bt[:, g * 128:(g + 1) * 128], xsm[:, 2, g * 512:(g + 1) * 512])], reads=[r_Bt, r_xs], psres=r_ps)
                gs = slice(g * 512, (g + 1) * 512)
                k.tt("dve", S_f[:, gs].rearrange("p (h d) -> p h d", d=64), S_f[:, gs].rearrange("p (h d) -> p h d", d=64),
                     bc(dec[:, 64 + g * 8:64 + (g + 1) * 8], 2, 64), ALU.mult, reads=[r_Sf, r_dec], writes=[r_Sf])
                k.tt("dve", S_f[:, gs], S_f[:, gs], ps[:], ALU.add, reads=[r_Sf, r_ps], writes=[r_Sf])
                k.act(S_fb[:, gs], S_f[:, gs], AF.Copy, reads=[r_Sf], writes=[r_Sfb])

        for t0 in range(T0, T0 + T, TT):
            win, r_win = winr.next()
            lo = t0 - 2
            hi = t0 + TT + 2
            a0 = max(lo, T0)
            a1 = min(hi, T0 + T)
            wrote = False
            if a0 > lo:
                k.op("dve", lambda e: e.memset(win[:, :, 0:a0 - lo], 0.0), writes=[r_win])
                wrote = True
            if a1 < hi:
                k.op("dve", lambda e: e.memset(win[:, :, TT + 4 - (hi - a1):TT + 4], 0.0),
                     writes=[] if wrote else [r_win], appends=[r_win] if wrote else [])
                wrote = True
            k.dma("sp", win[:, :, a0 - lo:a1 - lo], xbcT[:, a0:a1].rearrange("(c p) t -> p c t", p=128), reads=cr("xbcT", a0, a1),
                  appends=[r_win] if wrote else [], writes=[] if wrote else [r_win])
            xb, r_xb = xbr.next()
            for c in range(12):
                ps, r_ps = psf.next()
                k.mm(ps[:, 0:TT], [(dg5[:, c, kk, :], win[:, c, kk:kk + TT]) for kk in range(5)], reads=[r_win, r_dg5], psres=r_ps)
                k.act(xb[:, c, :], ps[:, 0:TT], AF.Silu, reads=[r_ps, r_vT], appends=[r_xb], bias=vT[:, V_CB + c:V_CB + c + 1])
            for s in range(NS):
                bg_step()
                ctx_ = front(t0, s, xb, r_xb)
                flush()
                pending[0] = ctx_
        flush()
        return sc

    def pass_S3b(l, seg, need_y, with_C=False):
        sc = Scope()
        S_f, S_b, S_fb, S_bb = SS["t"]
        T0, T = seg["T0"], seg["T"]
        nch = T // 128
        cth = make_C(l, seg, sc) if with_C else iter(())
        str_ = sc.ring(U("st"), [128, D], F32, 3)
        ctr = sc.ring(U("ct"), [128, 256], BF16, 3)
        edr = sc.ring(U("ed"), [128, 32], F32, 3)
        ypr = sc.ring(U("ypt"), [128, D], F32, 3)
        zr = sc.ring(U("zt"), [128, D], BF16, 3)
        junk = sc.sb(U("junk"), [128, 512], BF16)
        r_junk = Res()
        ssr = sc.ring(U("ss"), [128, 8], F32, 3)
        ynr = sc.ring(U("yn"), [128, D], BF16, 2)
        yTr = sc.ring(U("yT"), [128, 8, 128], BF16, 2)
        pend = [None]

        pend2 = [None]

        def stageB(c):
            c_idx, ypt, r_ypt, ss, r_ss = c
            k.act(ss[:, 2:4], ss[:, 0:2], AF.Sqrt, reads=[r_ss, r_const], appends=[r_ss], scale=1.0 / 512, bias=epsT[:, 0:1])
            k.op("dve", lambda e: e.reciprocal(ss[:, 4:6], ss[:, 2:4]), reads=[r_ss], appends=[r_ss])
            yn, r_yn = ynr.next()
            for g in range(2):
                gs = slice(g * 512, (g + 1) * 512)
                k.ts("dve", yn[:, gs], ypt[:, gs], ss[:, 4 + g:5 + g], None, ALU.mult, reads=[r_ypt, r_ss], appends=[r_yn])
            if pend2[0] is not None:
                stageB2(pend2[0])
                pend2[0] = None
            pt, r_pt = psb.next()
            k._deps("pe", [r_yn, r_const], (), [r_pt])
            for kc in range(8):
                k.op("pe", lambda e: e.transpose(pt[:, kc * 128:(kc + 1) * 128], yn[:, kc * 128:(kc + 1) * 128], ident_b[:]),
                     reads=[r_yn, r_const] if kc == 7 else (), appends=[r_pt] if kc == 7 else (), inc=(kc == 7))
            pend2[0] = (c_idx, pt, r_pt)

        def stageB2(c):
            c_idx, pt, r_pt = c
            yT, r_yT = yTr.next()
            k.tt("dve", yT[:], pt[:, :].rearrange("p (c t) -> p c t", c=8), bc(vT[:, V_SN:V_SN + 8], 2, 128), ALU.mult,
                 reads=[r_pt, r_vT], writes=[r_yT])
            k.dma("pool", ynT[c_idx], yT[:].rearrange("p c t -> p (c t)"), reads=[r_yT], writes=[R["ynT"][c_idx]])

        for cl in range(nch - 1, -1, -1):
            bg_step()
            next(cth, None)
            c_idx = T0 // 128 + cl
            stt_, r_st = str_.next()
            k.dma("sp", stt_[:], stb[c_idx], reads=[R["stb"][c_idx]], writes=[r_st])
            ed, r_ed = edr.next()
            k.dma("sp", ed[:], decd[c_idx], reads=[R["decd"][c_idx]], writes=[r_ed])
            pos = []
            if need_y:
                ct, r_ct = ctr.next()
                k.dma("sp", ct[:], ctd[c_idx], reads=[R["ctd"][c_idx]], writes=[r_ct])
                ypt, r_ypt = ypr.next()
                k.dma("sp", ypt[:], yp[c_idx * 128:(c_idx + 1) * 128, :], reads=[R["yp"][c_idx]], writes=[r_ypt])
                zt, r_zt = zr.next()
                k.dma("sp", zt[:], zs[c_idx * 128:(c_idx + 1) * 128, :], reads=[R["zs"][c_idx]], writes=[r_zt])
                for g in range(2):
                    gs = slice(g * 512, (g + 1) * 512)
                    po, r_po = psf.next()
                    k.mm(po[:], [(ct[:, g * 128:(g + 1) * 128], S_bb[:, gs])], reads=[r_ct, r_Sbb], psres=r_po)
                    pos.append((po, r_po))
            for g in range(2):
                gs = slice(g * 512, (g + 1) * 512)
                k.tt("dve", S_b[:, gs].rearrange("p (h d) -> p h d", d=64), S_b[:, gs].rearrange("p (h d) -> p h d", d=64),
                     bc(ed[:, 16 + g * 8:16 + (g + 1) * 8], 2, 64), ALU.mult, reads=[r_Sb, r_ed], writes=[r_Sb])
                k.tt("dve", S_b[:, gs], S_b[:, gs], stt_[:, gs], ALU.add, reads=[r_Sb, r_st], writes=[r_Sb])
                k.act(S_bb[:, gs], S_b[:, gs], AF.Copy, reads=[r_Sb], writes=[r_Sbb])
            if need_y:
                ss, r_ss = ssr.next()
                for g in range(2):
                    gs = slice(g * 512, (g + 1) * 512)
                    po, r_po = pos[g]
                    k.tt("dve", po[:].rearrange("p (h d) -> p h d", d=64), po[:].rearrange("p (h d) -> p h d", d=64),
                         bc(ed[:, g * 8:(g + 1) * 8], 2, 64), ALU.mult, reads=[r_ed], writes=[r_po])
                    k.tt("dve", ypt[:, gs], ypt[:, gs], po[:], ALU.add, reads=[r_po], writes=[r_ypt])
                    k.tt("dve", ypt[:, gs], ypt[:, gs], zt[:, gs], ALU.mult, reads=[r_zt], writes=[r_ypt])
                    k.act(junk[:], ypt[:, gs], AF.Square, reads=[r_ypt], writes=[r_junk], accum_out=ss[:, g:g + 1], appends=[r_ss])
                if pend[0] is not None:
                    stageB(pend[0])
                pend[0] = (c_idx, ypt, r_ypt, ss, r_ss)
        if pend[0] is not None:
            stageB(pend[0])
            pend[0] = None
        if pend2[0] is not None:
            stageB2(pend2[0])
            pend2[0] = None
        for _ in cth:
            pass
        return sc

    def make_C(l, seg, sc):
        T0, T, RL, TT = seg["T0"], seg["T"], seg["RL"], seg["TTP"]
        NRows = TT // RL
        dg = sc.sb(U("dg"), [128, 6, 31, 128], BF16)
        r_dg = Res()
        for c in range(6):
            for kk in range(31):
                k.act(dg[:, c, kk, :], ident_f[:], AF.Identity, reads=[r_const, r_vT], appends=[r_dg],
                      scale=vT[:, V_DW + c * 31 + kk:V_DW + c * 31 + kk + 1])
        cw = sc.sb(U("cw"), [128, 6, D], BF16)
        r_w = Res()
        k.dma("sp", cw[:], wb_co[l].rearrange("(kc p) n -> p kc n", p=128), reads=[RW["wb_co"]], writes=[r_w])
        cwr = sc.ring(U("cwin"), [128, 6, NRows, RL + 32], BF16, 2)
        for t_, r_ in cwr.bufs:
            k.op("pool", lambda e: e.memset(t_[:], 0.0), writes=[r_])
        acc = sc.sb(U("cacc"), [128, 6, TT], F32)
        r_acc = Res()
        sqr = sc.ring(U("sq"), [128, TT], F32, 2)
        stt_ = sc.sb(U("lnst"), [128, 4, TT], F32)
        r_ln = Res()
        sact = sc.sb(U("sact"), [128, 6, TT], BF16)
        r_sact = Res()
        gtr = sc.ring(U("gt"), [128, 8, TT], BF16, 2)
        mcr = sc.ring(U("mc"), [128, TT], F32, 3)

        def ctile(t0):
            gt, r_gt = gtr.next()
            k.dma("sp", gt[:], gT[D:2 * D, t0:t0 + TT].rearrange("(c p) t -> p c t", p=128), reads=cr("gT", t0, t0 + TT), writes=[r_gt])
            cwin, r_cw = cwr.next()
            for c in range(6):
                k.dma("sp", cwin[:, c, :, 16:16 + RL], cvuT[c * 128:(c + 1) * 128, t0:t0 + TT].rearrange("p (r w) -> p r w", w=RL),
                      reads=cr("cvuT", t0, t0 + TT), appends=[r_cw])
            for c in range(6):
                ps, r_ps = psf.next()
                k.mm(ps[:, 0:TT].rearrange("p (r w) -> p r w", w=RL),
                     [(dg[:, c, kk, :], cwin[:, c, :, kk + 1:kk + 1 + RL]) for kk in range(31)], reads=[r_cw, r_dg], psres=r_ps)
                k.act(acc[:, c, :], ps[:, 0:TT], AF.Identity, reads=[r_ps, r_vT], appends=[r_acc], bias=vT[:, V_DB + c:V_DB + c + 1])
                if c == 2:
                    yield
            p1, r_p1 = psf.next()
            k.mm(p1[:, 0:TT], [(ones_f[:], acc[:, c, :]) for c in range(6)], reads=[r_acc, r_const], psres=r_p1)
            p2, r_p2 = psf.next()
            for c in range(6):
                sq, r_sq = sqr.next()
                k.act(sq[:], acc[:, c, :], AF.Square, reads=[r_acc], writes=[r_sq])
                k.op("pe", lambda e: e.matmul(p2[:, 0:TT], lhsT=ones_f[:], rhs=sq[:], start=(c == 0), stop=(c == 5)),
                     reads=[r_sq, r_const], appends=[r_p2])
            yield
            k.act(stt_[:, 0, :], p1[:, 0:TT], AF.Copy, reads=[r_p1], writes=[r_ln], scale=1.0 / 768)
            k.tt("dve", stt_[:, 1, :], stt_[:, 0, :], stt_[:, 0, :], ALU.mult, reads=[r_ln], writes=[r_ln])
            k.stt("dve", stt_[:, 1, :], p2[:, 0:TT], 1.0 / 768, stt_[:, 1, :], ALU.mult, ALU.subtract, reads=[r_p2, r_ln], writes=[r_ln])
            k.act(stt_[:, 3, :], stt_[:, 1, :], AF.Sqrt, reads=[r_ln, r_const], writes=[r_ln], bias=epsT[:, 0:1])
            k.op("dve", lambda e: e.reciprocal(stt_[:, 2, :], stt_[:, 3, :]), reads=[r_ln], writes=[r_ln])
            for c in range(6):
                k.tt("dve", acc[:, c, :], acc[:, c, :], stt_[:, 0, :], ALU.subtract, reads=[r_ln, r_acc], writes=[r_acc])
                k.tt("dve", acc[:, c, :], acc[:, c, :], stt_[:, 2, :], ALU.mult, reads=[r_ln, r_acc], writes=[r_acc])
                k.act(sact[:, c, :], acc[:, c, :], AF.Silu, reads=[r_acc, r_vT], appends=[r_sact],
                      scale=vT[:, V_LG + c:V_LG + c + 1], bias=vT[:, V_LB + c:V_LB + c + 1])
            yield
            for jc in range(8):
                ps, r_ps = psf.next()
                k.mm(ps[:, 0:TT], [(cw[:, kc, jc * 128:(jc + 1) * 128], sact[:, kc, :]) for kc in range(6)], reads=[r_w, r_sact], psres=r_ps)
                mc, r_mc = mcr.next()
                k.tt("dve", mc[:], ps[:, 0:TT], gt[:, jc, :], ALU.mult, reads=[r_ps, r_gt], writes=[r_mc])
                k.dma("pool", mconv[jc * 128:(jc + 1) * 128, t0:t0 + TT], mc[:], reads=[r_mc], appends=cr("mconv", t0, t0 + TT))
        def allparts():
            for t0 in range(T0, T0 + T, TT):
                for _ in ctile(t0):
                    yield
                yield
        return allparts()


    def pass_M(l, seg):
        sc = Scope()
        T0, T, RL, W, NRT, j = seg["T0"], seg["T"], seg["RL"], seg["W"], seg["NR"], seg["j"]
        TT = 256
        NRows = TT // RL
        PR = TT // W
        invc = invc_l if W == 64 else invc_c
        sw = sc.sb(U("sw"), [128, 8, D], BF16)
        pw = sc.sb(U("pw"), [96, 8, 256], BF16)
        wo = sc.sb(U("wo"), [128, 8, D], BF16)
        r_w = Res()
        k.dma("sp", sw[:], wb_so[l].rearrange("(kc p) n -> p kc n", p=128), reads=[RW["wb_so"]], appends=[r_w])
        k.dma("sp", pw[:], wb_pw[l].rearrange("(q p) n -> p q n", p=96), reads=[RW["wb_pw"]], appends=[r_w])
        k.dma("sp", wo[:], wb_wo[l].rearrange("(kc p) n -> p kc n", p=128), reads=[RW["wb_wo"]], appends=[r_w])
        mcr2 = sc.ring(U("mct"), [128, 8, TT], F32, 2)
        gtr = sc.ring(U("gt"), [128, 24, TT], BF16, 2)
        pxr = sc.ring(U("px"), [96, 2, PR + 15, W], F32, 2)
        psa_ = sc.sb(U("psa"), [96, 2, PR + 15, W], F32)
        psb2 = sc.sb(U("psb2"), [96, 2, PR + 15, W], F32)
        r_pA, r_pB2 = Res(), Res()
        pp = sc.sb(U("pp"), [96, 8, TT], BF16)
        r_pp = Res()
        ynr = sc.ring(U("ynt"), [128, 2, 8, 128], BF16, 2)
        mg = sc.sb(U("mg"), [128, 8, TT], F32)
        r_mg = Res()
        tmpr = sc.ring(U("mtmp"), [128, TT], F32, 2)
        mT = sc.sb(U("mT"), [128, 8, TT], BF16)
        r_mT = Res()
        xtr = sc.ring(U("xt"), [128, 2, D], F32, 2)
        t2r = sc.ring(U("t2"), [128, 512], F32, 2)
        mB = sc.sb(U("mB"), [128, D])
        r_modB = Res()
        k.dma("sp", mB[:], modD[l, j, 2 * D:3 * D].partition_broadcast(128), reads=[RW["modD"]], writes=[r_modB])
        for t0 in range(T0, T0 + T, TT):
            gt, r_gt = gtr.next()
            k.dma("sp", gt[:], gT[:, t0:t0 + TT].rearrange("(c p) t -> p c t", p=128), reads=cr("gT", t0, t0 + TT), writes=[r_gt])
            ynt, r_ynt = ynr.next()
            k.dma("sp", ynt[:].rearrange("p n c t -> p n (c t)"), ynT[t0 // 128:t0 // 128 + 2].rearrange("n p f -> p n f"), reads=cr("ynT", t0, t0 + TT), writes=[r_ynt])
            for jc in range(8):
                ps, r_ps = psf.next()
                k.mm(ps[:, 0:TT].rearrange("p (n t) -> p n t", n=2), [(sw[:, kc, jc * 128:(jc + 1) * 128], ynt[:, :, kc, :]) for kc in range(8)], reads=[r_w, r_ynt], psres=r_ps)
                k.tt("dve", mg[:, jc, :], ps[:, 0:TT], gt[:, jc, :], ALU.mult, reads=[r_ps, r_gt], appends=[r_mg])
            mct, r_mct = mcr2.next()
            k.dma("sp", mct[:], mconv[:, t0:t0 + TT].rearrange("(c p) t -> p c t", p=128), reads=cr("mconv", t0, t0 + TT), writes=[r_mct])
            for jc in range(8):
                k.tt("dve", mg[:, jc, :], mg[:, jc, :], mct[:, jc, :], ALU.add, reads=[r_mct, r_mg], writes=[r_mg])
            prow0 = (t0 - T0) // W
            for g, w in enumerate(POOLW):
                pxb, r_px = pxr.next()
                lo_r = prow0 - 8
                hi_r = prow0 + PR + 7
                a0 = max(lo_r, 0)
                a1 = min(hi_r, NRT)
                wrote = False
                if a0 > lo_r:
                    k.op("pool", lambda e: e.memset(pxb[:, :, 0:a0 - lo_r, :], 0.0), writes=[r_px])
                    wrote = True
                if a1 < hi_r:
                    k.op("pool", lambda e: e.memset(pxb[:, :, PR + 15 - (hi_r - a1):PR + 15, :], 0.0),
                         writes=[] if wrote else [r_px], appends=[r_px] if wrote else [])
                    wrote = True
                ta, tb = T0 + a0 * W, T0 + a1 * W
                for q2 in range(2):
                    q = g * 2 + q2
                    k.dma("sp", pxb[:, q2, a0 - lo_r:a1 - lo_r, :], plT[q * 96:(q + 1) * 96, ta:tb].rearrange("p (r w) -> p r w", w=W),
                          reads=cr("plT", ta, tb), appends=[r_px] if (wrote or q2 > 0) else [], writes=[] if (wrote or q2 > 0) else [r_px])
                lo_b = 8 - w // 2
                src, r_src = pxb, r_px
                step = 1
                dsts = [(psa_, r_pA), (psb2, r_pB2)]
                di = 0
                while step < w:
                    rem = w // (2 * step)
                    hi_need = 8 + PR - w // 2 + (w - 2 * step)
                    dst, r_dst = dsts[di]
                    di ^= 1
                    k.tt("dve", dst[:, :, lo_b:hi_need, :], src[:, :, lo_b:hi_need, :], src[:, :, lo_b + step:hi_need + step, :], ALU.add,
                         reads=[r_src], writes=[r_dst])
                    src, r_src = dst, r_dst
                    step *= 2
                ic = invc[0:96, g, prow0:prow0 + PR]
                dst, r_dst = dsts[di]
                k.tt("dve", dst[:, :, 8:8 + PR, :], src[:, :, lo_b:lo_b + PR, :], bc(bc(ic, 1, 2), 3, W), ALU.mult,
                     reads=[r_src, r_const], writes=[r_dst])
                k.tt("dve", pp[:, g * 2:g * 2 + 2, :].rearrange("p q (r w) -> p q r w", w=W), dst[:, :, 8:8 + PR, :], pxb[:, :, 8:8 + PR, :], ALU.subtract,
                     reads=[r_dst, r_px], appends=[r_pp])
                if ppd is not None:
                    for q2 in range(2):
                        q = g * 2 + q2
                        k.dma("pool", ppd[q * 96:(q + 1) * 96, t0:t0 + TT], pp[:, q, :], reads=[r_pp], writes=[Res()])
                for o2 in range(2):
                    jc = g * 2 + o2
                    ps, r_ps = psf.next()
                    k.mm(ps[:, 0:TT], [(pw[:, g * 2 + q2, o2 * 128:(o2 + 1) * 128], pp[:, g * 2 + q2, :]) for q2 in range(2)],
                         reads=[r_w, r_pp], psres=r_ps)
                    tm, r_tm = tmpr.next()
                    k.stt("dve", tm[:], ps[:, 0:TT], vT[:, V_PS + jc:V_PS + jc + 1], gt[:, 16 + jc, :], ALU.mult, ALU.mult,
                          reads=[r_ps, r_gt, r_vT], writes=[r_tm])
                    k.tt("dve", mT[:, jc, :], mg[:, jc, :], tm[:], ALU.add, reads=[r_tm, r_mg], appends=[r_mT])
            xt, r_xt = xtr.next()
            k.dma("sp", xt[:], xs[t0:t0 + TT, :].rearrange("(s p) d -> p s d", p=128), reads=cr("xs", t0, t0 + TT), writes=[r_xt])
            for s in range(2):
                for half in range(2):
                    ps, r_ps = psf.next()
                    k.mm(ps[:], [(mT[:, kc, s * 128:(s + 1) * 128], wo[:, kc, half * 512:(half + 1) * 512]) for kc in range(8)],
                         reads=[r_w, r_mT], psres=r_ps)
                    t2, r_t2 = t2r.next()
                    hs = slice(half * 512, (half + 1) * 512)
                    k.tt("dve", t2[:], ps[:], mB[:, hs], ALU.mult, reads=[r_ps, r_modB], writes=[r_t2])
                    k.tt("dve", xt[:, s, hs], xt[:, s, hs], t2[:], ALU.add, reads=[r_t2], writes=[r_xt])
            k.dma("pool", xs[t0:t0 + TT, :].rearrange("(s p) d -> p s d", p=128), xt[:], reads=[r_xt], writes=cr("xs", t0, t0 + TT))
        return sc

    def pass_F(l, seg, final):
        sc = Scope()
        T0, T, j = seg["T0"], seg["T"], seg["j"]
        TT = 256
        NJ = FFH // 128
        wg = sc.sb(U("wg"), [128, 8, FFH], BF16)
        wu = sc.sb(U("wu"), [128, 8, FFH], BF16)
        wd = sc.sb(U("wd"), [128, NJ, D], BF16)
        r_w = Res()
        for kc in range(8):
            k.dma("sp", wg[:, kc, :], wb_fg[l, kc * 128:(kc + 1) * 128, :], reads=[RW["wb_fg"]], appends=[r_w])
            k.dma("sp", wu[:, kc, :], wb_fu[l, kc * 128:(kc + 1) * 128, :], reads=[RW["wb_fu"]], appends=[r_w])
        k.dma("sp", wd[:], wb_fd[l].rearrange("(kc p) n -> p kc n", p=128), reads=[RW["wb_fd"]], appends=[r_w])
        xtr = sc.ring(U("xt"), [128, 2, D], F32, 2)
        hTr = sc.ring(U("hTt"), [128, 8, TT], BF16, 2)
        junk = sc.sb(U("junk"), [128, D], BF16)
        r_junk = Res()
        nb = (junk, r_junk, sc.ring(U("ss"), [128, 4], F32, 4), sc.ring(U("xn"), [128, D], BF16, 2))
        a_t = sc.sb(U("act"), [128, NJ, TT], BF16)
        r_a = Res()
        sgr = sc.ring(U("sg"), [128, TT], F32, 2)
        t2r = sc.ring(U("t2"), [128, 512], F32, 2)
        mB = sc.sb(U("mB"), [128, D])
        r_modB = Res()
        k.dma("sp", mB[:], modD[l, j, 5 * D:6 * D].partition_broadcast(128), reads=[RW["modD"]], writes=[r_modB])
        for t0 in range(T0, T0 + T, TT):
            xt, r_xt = xtr.next()
            k.dma("sp", xt[:], xs[t0:t0 + TT, :].rearrange("(s p) d -> p s d", p=128), reads=cr("xs", t0, t0 + TT), writes=[r_xt])
            hT, r_hT = hTr.next()
            for s in range(2):
                norm_T(nb, xt[:, s, :], r_xt, j, 2, lambda kc: hT[:, kc, s * 128:(s + 1) * 128], r_hT)
            for jc in range(NJ):
                pg, r_pg = psf.next()
                k.mm(pg[:, 0:TT], [(wg[:, kc, jc * 128:(jc + 1) * 128], hT[:, kc, :]) for kc in range(8)], reads=[r_w, r_hT], psres=r_pg)
                sg_, r_sg = sgr.next()
                k.act(sg_[:], pg[:, 0:TT], AF.Silu, reads=[r_pg], writes=[r_sg])
                pu, r_pu = psf.next()
                k.mm(pu[:, 0:TT], [(wu[:, kc, jc * 128:(jc + 1) * 128], hT[:, kc, :]) for kc in range(8)], reads=[r_w, r_hT], psres=r_pu)
                k.tt("dve", a_t[:, jc, :], pu[:, 0:TT], sg_[:], ALU.mult, reads=[r_pu, r_sg], appends=[r_a])
            for s in range(2):
                for half in range(2):
                    ps, r_ps = psf.next()
                    k.mm(ps[:], [(a_t[:, jc, s * 128:(s + 1) * 128], wd[:, jc, half * 512:(half + 1) * 512]) for jc in range(NJ)],
                         reads=[r_w, r_a], psres=r_ps)
                    t2, r_t2 = t2r.next()
                    hs = slice(half * 512, (half + 1) * 512)
                    k.tt("dve", t2[:], ps[:], mB[:, hs], ALU.mult, reads=[r_ps, r_modB], writes=[r_t2])
                    k.tt("dve", xt[:, s, hs], xt[:, s, hs], t2[:], ALU.add, reads=[r_t2], writes=[r_xt])
            if not final:
                k.dma("pool", xs[t0:t0 + TT, :].rearrange("(s p) d -> p s d", p=128), xt[:], reads=[r_xt], writes=cr("xs", t0, t0 + TT))
            else:
                ssq = nb[2]
                for s in range(2):
                    ss, r_ss = ssq.next()
                    k.act(junk[:], xt[:, s, :], AF.Square, reads=[r_xt], writes=[r_junk], accum_out=ss[:, 0:1], appends=[r_ss])
                    k.act(ss[:, 1:2], ss[:, 0:1], AF.Sqrt, reads=[r_ss, r_const], appends=[r_ss], scale=1.0 / D, bias=epsT[:, 0:1])
                    k.op("dve", lambda e: e.reciprocal(ss[:, 2:3], ss[:, 1:2]), reads=[r_ss], appends=[r_ss])
                    k.stt("dve", xt[:, s, :], xt[:, s, :], ss[:, 2:3], fgB[:], ALU.mult, ALU.mult, reads=[r_ss, r_fgB], writes=[r_xt])
                o0 = t0 - T0
                k.dma("pool", out[o0:o0 + TT, :].rearrange("(s p) d -> p s d", p=128), xt[:], reads=[r_xt], writes=[Res()])
        return sc

    def run_pass(fn, *a):
        sc = fn(*a)
        k.barrier()
        sc.close()

    for l in range(n_layers):
        last = l == NL - 1
        RW.clear()
        RW.update(RWL[l])
        run_pass(layer_setup, l)
        ssc = Scope()
        S_f = ssc.sb(U("S_f"), [128, D])
        S_b = ssc.sb(U("S_b"), [128, D])
        S_fb = ssc.sb(U("S_fb"), [128, D], BF16)
        S_bb = ssc.sb(U("S_bb"), [128, D], BF16)
        SS["t"] = (S_f, S_b, S_fb, S_bb)
        k.op("dve", lambda e: e.memset(S_f[:], 0.0), writes=[r_Sf])
        k.op("dve", lambda e: e.memset(S_b[:], 0.0), writes=[r_Sb])
        k.op("dve", lambda e: e.memset(S_fb[:], 0.0), writes=[r_Sfb])
        k.op("dve", lambda e: e.memset(S_bb[:], 0.0), writes=[r_Sbb])
        cseg, lseg = SEGS
        run_pass(pass_P1, l, cseg)
        if not last:
            run_pass(pass_P2, l, cseg)
        run_pass(pass_S3a, l, cseg, not last)
        run_pass(pass_S3b, l, cseg, not last, not last)
        run_pass(pass_P1, l, lseg)
        run_pass(pass_P2, l, lseg)
        run_pass(pass_S3a, l, lseg, True)
        run_pass(pass_S3b, l, lseg, True, True)
        ssc.close()
        if not last:
            run_pass(pass_M, l, cseg)
            if "F" not in skip:
                run_pass(pass_F, l, cseg, False)
        run_pass(pass_M, l, lseg)
        if "F" not in skip:
            run_pass(pass_F, l, lseg, last)
        bg_step(len(bgq))

    for e in ("sp", "pool"):
        k.drain(e)
    return nc, k


_CACHE = {}


def _host_inputs(inputs, b):
    f = lambda a: np.ascontiguousarray(np.asarray(a, dtype=np.float32))
    c = f(inputs["c"])[b]
    cc = f(inputs["c_ctx"])
    cT = np.stack([c.reshape(8, 128).T, cc.reshape(8, 128).T], axis=-1)
    vts = []
    for l in range(NL):
        cols = []

        def fm(v):
            return f(v).reshape(-1, 128).T
        cols.append(fm(inputs["norm1_g"][l]))
        cols.append(fm(inputs["norm2_g"][l]))
        cols.append(fm(inputs["ssd_norm_g"][l]))
        cols.append(fm(inputs["pool_scale"][l]))
        cols.append(fm(inputs["ssd_conv_b"][l]))
        cw = f(inputs["ssd_conv_w"][l])
        cols.append(cw.reshape(5, 12, 128).transpose(2, 1, 0).reshape(128, 60))
        cols.append(fm(inputs["cv_dw_b"][l]))
        cols.append(fm(inputs["cv_ln_g"][l]))
        cols.append(fm(inputs["cv_ln_b"][l]))
        dw = f(inputs["cv_dw_w"][l])
        cols.append(dw.reshape(31, 6, 128).transpose(2, 1, 0).reshape(128, 186))
        vts.append(np.concatenate(cols, axis=1))
    vT = np.stack(vts, 0)
    assert vT.shape == (NL, 128, NV)
    b_adaT = f(inputs["b_ada"]).reshape(NL, 48, 128).transpose(0, 2, 1)
    m = dict(
        x=f(inputs["x"][b]), ctx=f(inputs["ctx"][b]), cT=f(cT), w_ada=f(inputs["w_ada"]), b_ada=f(inputs["b_ada"]),
        b_adaT=f(b_adaT), vT=f(vT), ssd_a_log=f(inputs["ssd_a_log"]).reshape(NL, 32),
        ssd_dt_bias=f(inputs["ssd_dt_bias"]).reshape(NL, 32), ssd_d=f(inputs["ssd_d"]), final_g=f(inputs["final_g"]),
        w_in=f(inputs["w_in"]), ssd_w_out=f(inputs["ssd_w_out"]), cv_w_out=f(inputs["cv_w_out"]), pool_w=f(inputs["pool_w"]),
        w_out=f(inputs["w_out"]), ffn_w_gate=f(inputs["ffn_w_gate"]), ffn_w_up=f(inputs["ffn_w_up"]), ffn_w_down=f(inputs["ffn_w_down"]),
    )
    return m


def kernel(**inputs):
    if "nc" not in _CACHE:
        _CACHE["nc"] = build()[0]
    nc = _CACHE["nc"]
    in_maps = [_host_inputs(inputs, b) for b in range(8)]
    res = run_bass_kernel_spmd(nc, in_maps, core_ids=list(range(8)))
    return np.stack([np.asarray(r["out"], dtype=np.float32) for r in res.results], axis=0)
```
